# Optimizing a Trainium2 kernel written in Bass

```python
import math
import jax, jax.numpy as jnp
from jax import lax
import numpy as np

D_MODEL = 2048
BATCH = 1
SEQ = 8192
DEPTH = 4

GRID_W = 64
MIX_WIDTH = D_MODEL
DIFF_HEADS = 8
DIFF_QK_DIM = MIX_WIDTH // (4 * DIFF_HEADS)
DIFF_V_DIM = 2 * DIFF_QK_DIM
DIFF_W = DIFF_HEADS * DIFF_V_DIM
NA_HEADS = 8
NA_HEAD_DIM = MIX_WIDTH // (2 * NA_HEADS)
NA_W = NA_HEADS * NA_HEAD_DIM
NA_WIN_ROWS = 8
NA_WIN_COLS = 16
IN_COLS = 3 * DIFF_W + 3 * NA_W
D_FF = 4 * D_MODEL
ROPE_THETA = 10000.0
EPS = 1e-6
Q_BLOCK = 128

kernel_name = "hybrid_diffattn_natten_encoder"


def rmsnorm(x, g):
    xf = x.astype(jnp.float32)
    y = xf * lax.rsqrt(jnp.mean(xf * xf, axis=-1, keepdims=True) + EPS)
    return (y * g.astype(jnp.float32)).astype(x.dtype)


def rope_tables(seq, dim):
    inv_freq = 1.0 / (ROPE_THETA ** (jnp.arange(0, dim, 2, dtype=jnp.float32) / dim))
    ang = jnp.arange(seq, dtype=jnp.float32)[:, None] * inv_freq[None, :]
    return jnp.cos(ang), jnp.sin(ang)


def apply_rope(x, cos, sin):
    xf = x.astype(jnp.float32)
    x1, x2 = jnp.split(xf, 2, axis=-1)
    c = cos[None, :, None, None, :]
    s = sin[None, :, None, None, :]
    return jnp.concatenate([x1 * c - x2 * s, x2 * c + x1 * s], axis=-1).astype(x.dtype)


def diff_attention(q, k, v, lam, lambda_init, subln_g):
    b, s, h, _, d = q.shape
    nblk = s // Q_BLOCK
    scale = d ** -0.5
    qb = jnp.moveaxis(q.reshape(b, nblk, Q_BLOCK, h, 2, d), 1, 0)

    def block(qi):
        sc = jnp.einsum('bqhmd,bkhmd->bhmqk', qi, k).astype(jnp.float32) * scale
        p = jax.nn.softmax(sc, axis=-1)
        w = (p[:, :, 0] - lam * p[:, :, 1]).astype(v.dtype)
        return jnp.einsum('bhqk,bkhe->bqhe', w, v)

    o = lax.map(block, qb)
    o = jnp.moveaxis(o, 0, 1).reshape(b, s, h, DIFF_V_DIM)
    o = rmsnorm(o, subln_g) * (1.0 - lambda_init)
    return o.reshape(b, s, h * DIFF_V_DIM)


def neighbourhood_attention(q, k, v, rpb):
    b, s, h, d = q.shape
    rows = s // GRID_W
    kr = min(NA_WIN_ROWS, rows)
    kc = NA_WIN_COLS
    scale = d ** -0.5
    qg = q.reshape(b, rows, GRID_W, h, d)
    kg = k.reshape(b, rows, GRID_W, h, d)
    vg = v.reshape(b, rows, GRID_W, h, d)
    c = np.arange(GRID_W)
    cs = np.clip(c - kc // 2, 0, GRID_W - kc)
    col_idx = cs[:, None] + np.arange(kc)
    col_rel = col_idx - c[:, None] + (NA_WIN_COLS - 1)

    def row_block(r):
        rs = jnp.clip(r - kr // 2, 0, rows - kr)
        q_r = lax.dynamic_index_in_dim(qg, r, axis=1, keepdims=False)
        k_rows = lax.dynamic_slice_in_dim(kg, rs, kr, axis=1)
        v_rows = lax.dynamic_slice_in_dim(vg, rs, kr, axis=1)
        k_nb = k_rows[:, :, col_idx]
        v_nb = v_rows[:, :, col_idx]
        row_rel = rs + jnp.arange(kr) - r + (NA_WIN_ROWS - 1)
        bias = jnp.take(rpb, row_rel, axis=1)[:, :, col_rel]
        bias = jnp.transpose(bias, (0, 2, 1, 3)).astype(jnp.float32)
        sc = jnp.einsum('bchd,brcjhd->bhcrj', q_r, k_nb).astype(jnp.float32) * scale + bias[None]
        p = jax.nn.softmax(sc.reshape(b, h, GRID_W, kr * kc), axis=-1).reshape(b, h, GRID_W, kr, kc)
        return jnp.einsum('bhcrj,brcjhd->bchd', p.astype(v.dtype), v_nb)

    o = lax.map(row_block, jnp.arange(rows))
    return jnp.moveaxis(o, 0, 1).reshape(b, s, h * d)


def setup_inputs(seed: int = 0) -> dict:
    key = jax.random.key(seed)
    ks = jax.random.split(key, 15)
    f32 = jnp.float32
    nrm = lambda k, shape, sc: jax.random.normal(k, shape, f32) * sc
    return {
        "x": nrm(ks[0], (BATCH, SEQ, D_MODEL), 1.0),
        "attn_norm": 1.0 + nrm(ks[1], (DEPTH, D_MODEL), 0.05),
        "w_in": nrm(ks[2], (DEPTH, D_MODEL, IN_COLS), D_MODEL ** -0.5),
        "lambda_q1": nrm(ks[3], (DEPTH, DIFF_QK_DIM), 0.1),
        "lambda_k1": nrm(ks[4], (DEPTH, DIFF_QK_DIM), 0.1),
        "lambda_q2": nrm(ks[5], (DEPTH, DIFF_QK_DIM), 0.1),
        "lambda_k2": nrm(ks[6], (DEPTH, DIFF_QK_DIM), 0.1),
        "diff_subln": 1.0 + nrm(ks[7], (DEPTH, DIFF_V_DIM), 0.05),
        "na_norm": 1.0 + nrm(ks[8], (DEPTH, NA_W), 0.05),
        "na_rpb": nrm(ks[9], (DEPTH, NA_HEADS, 2 * NA_WIN_ROWS - 1, 2 * NA_WIN_COLS - 1), 0.1),
        "w_out": nrm(ks[10], (DEPTH, MIX_WIDTH, D_MODEL), MIX_WIDTH ** -0.5),
        "mlp_norm": 1.0 + nrm(ks[11], (DEPTH, D_MODEL), 0.05),
        "w_mlp_in": nrm(ks[12], (DEPTH, D_MODEL, D_FF), D_MODEL ** -0.5),
        "w_mlp_out": nrm(ks[13], (DEPTH, D_FF, D_MODEL), D_FF ** -0.5),
        "final_norm": 1.0 + nrm(ks[14], (D_MODEL,), 0.05),
    }


def reference(x, attn_norm, w_in, lambda_q1, lambda_k1, lambda_q2, lambda_k2, diff_subln,
              na_norm, na_rpb, w_out, mlp_norm, w_mlp_in, w_mlp_out, final_norm):
    b, s, _ = x.shape
    cos, sin = rope_tables(s, DIFF_QK_DIM)
    split_pts = [DIFF_W, 2 * DIFF_W, 3 * DIFF_W, 3 * DIFF_W + NA_W, 3 * DIFF_W + 2 * NA_W]
    for l in range(DEPTH):
        lambda_init = 0.8 - 0.6 * math.exp(-0.3 * l)
        h = rmsnorm(x, attn_norm[l])
        proj = jnp.einsum('bsd,dc->bsc', h, w_in[l])
        qd, kd, vd, qn, kn, vn = jnp.split(proj, split_pts, axis=-1)
        qd = apply_rope(qd.reshape(b, s, DIFF_HEADS, 2, DIFF_QK_DIM), cos, sin)
        kd = apply_rope(kd.reshape(b, s, DIFF_HEADS, 2, DIFF_QK_DIM), cos, sin)
        vd = vd.reshape(b, s, DIFF_HEADS, DIFF_V_DIM)
        lam = (jnp.exp(jnp.sum(lambda_q1[l].astype(jnp.float32) * lambda_k1[l].astype(jnp.float32)))
               - jnp.exp(jnp.sum(lambda_q2[l].astype(jnp.float32) * lambda_k2[l].astype(jnp.float32)))
               + lambda_init)
        o_diff = diff_attention(qd, kd, vd, lam, lambda_init, diff_subln[l])
        o_na = neighbourhood_attention(qn.reshape(b, s, NA_HEADS, NA_HEAD_DIM),
                                       kn.reshape(b, s, NA_HEADS, NA_HEAD_DIM),
                                       vn.reshape(b, s, NA_HEADS, NA_HEAD_DIM),
                                       na_rpb[l])
        o_na = rmsnorm(o_na, na_norm[l])
        mix = jnp.concatenate([o_diff, o_na], axis=-1)
        x = x + jnp.einsum('bsc,cd->bsd', mix, w_out[l])
        h = rmsnorm(x, mlp_norm[l])
        u = jax.nn.relu(jnp.einsum('bsd,df->bsf', h, w_mlp_in[l]))
        x = x + jnp.einsum('bsf,fd->bsd', u * u, w_mlp_out[l])
    return rmsnorm(x, final_norm)
```

```python
import contextlib
import math
import numpy as np
import ml_dtypes
import concourse.bass as bass
import concourse.mybir as mybir
from concourse.bass_utils import run_bass_kernel_spmd

F32 = mybir.dt.float32
BF16 = mybir.dt.bfloat16
AF = mybir.ActivationFunctionType
ALU = mybir.AluOpType
ENGS = ["pe", "act", "dve", "pool", "sp"]

NCORES = 8
D = 2048
SEQ = 8192
TOK = 1024
DEPTH = 4
EPS = 1e-6
NEG = -30000.0
NBLK = 30
B0 = 18


class _Op:
    __slots__ = ("eng", "fn", "deps", "marked", "count", "dma_key", "dma_target", "idx", "grp")


class Prog:
    def __init__(self, nc):
        self.nc = nc
        self.ops = {e: [] for e in ENGS}
        self.last_w = {}
        self.readers = {}
        self.dma_cum = {}
        self.stack = contextlib.ExitStack()
        self.finals = []

    def sb(self, name, shape, dt):
        return self.stack.enter_context(self.nc.sbuf_tensor(name, list(shape), dt))

    def ps(self, name, shape, dt=F32):
        return self.stack.enter_context(self.nc.psum_tensor(name, list(shape), dt))

    def op(self, eng, fn, reads=(), writes=(), dma_key=None, grp=None):
        o = _Op()
        o.grp = grp
        o.eng = eng
        o.fn = fn
        deps = []
        for r in reads:
            w = self.last_w.get(r)
            if w is not None:
                deps.append(w)
        for r in writes:
            w = self.last_w.get(r)
            if w is not None:
                deps.append(w)
            deps.extend(self.readers.get(r, ()))
        o.deps = deps
        o.marked = False
        o.count = 0
        o.dma_key = dma_key
        o.dma_target = 0
        if dma_key is not None:
            self.dma_cum[dma_key] = self.dma_cum.get(dma_key, 0) + 16
            o.dma_target = self.dma_cum[dma_key]
        o.idx = len(self.ops[eng])
        self.ops[eng].append(o)
        for r in reads:
            self.readers.setdefault(r, []).append(o)
        for r in writes:
            self.last_w[r] = o
            self.readers[r] = []
        return o

    def dma(self, out, in_, reads=(), writes=(), key=None, eng="sp", grp=None):
        return self.op(eng, lambda e: e.dma_start(out=out, in_=in_), reads, writes, dma_key=key, grp=grp)

    def emit(self):
        nc = self.nc
        final_waits = self.finals
        for e in ENGS:
            for o in self.ops[e]:
                for d in o.deps:
                    if d.dma_key is None and (d.eng != o.eng or o.eng != "pe"):
                        d.marked = True
        for d in final_waits:
            if d.dma_key is None:
                d.marked = True
        gmax = {}
        for e in ENGS:
            c = 0
            for o in self.ops[e]:
                if o.marked:
                    c += 1
                    o.count = c
                if o.dma_key is not None and o.grp is not None:
                    gk = (o.dma_key, o.grp)
                    gmax[gk] = max(gmax.get(gk, 0), o.dma_target)
        self._own_target = {}
        for e in ENGS:
            for o in self.ops[e]:
                if o.dma_key is not None:
                    self._own_target[id(o)] = o.dma_target
                    if o.grp is not None:
                        o.dma_target = gmax[(o.dma_key, o.grp)]
        sems = {}
        for e in ENGS:
            sems[("eng", e)] = self.stack.enter_context(nc.semaphore("s_" + e))
        for k in self.dma_cum:
            sems[("dma", k)] = self.stack.enter_context(nc.semaphore("d_" + str(k)))
        block = self.stack.enter_context(nc.Block())

        def run(ename, eng):
            waited = {}
            for o in self.ops[ename]:
                need = {}
                for d in o.deps:
                    if d.dma_key is not None:
                        k = ("dma", d.dma_key)
                        v = d.dma_target
                    elif d.eng != ename or ename != "pe":
                        k = ("eng", d.eng)
                        v = d.count
                    else:
                        continue
                    if v > need.get(k, 0):
                        need[k] = v
                for k, v in need.items():
                    if waited.get(k, 0) < v:
                        eng.wait_ge(sems[k], v)
                        waited[k] = v
                ins = o.fn(eng)
                if o.dma_key is not None:
                    ins.then_inc(sems[("dma", o.dma_key)], 16)
                elif o.marked:
                    ins.then_inc(sems[("eng", ename)], 1)
            if ename == "sp":
                fin = {}
                for d in final_waits:
                    if d.dma_key is not None:
                        k, v = ("dma", d.dma_key), d.dma_target
                    else:
                        k, v = ("eng", d.eng), d.count
                    fin[k] = max(fin.get(k, 0), v)
                for k, v in fin.items():
                    eng.wait_ge(sems[k], v)

        @block.tensor
        def _(e):
            run("pe", e)

        @block.scalar
        def _(e):
            run("act", e)

        @block.vector
        def _(e):
            run("dve", e)

        @block.gpsimd
        def _(e):
            run("pool", e)

        @block.sync
        def _(e):
            run("sp", e)

        self.stack.close()


class Ctx:
    def __init__(self, P, nc, n_scratch=5, with_bf_consts=True):
        self.P = P
        self.nc = nc
        self.pb = [P.ps(f"pb{i}", [128, 512]) for i in range(8)]
        self.bank_rr = 0
        self.scr = [P.sb(f"scr{i}", [128, 512], F32) for i in range(n_scratch)]
        self.scr_rr = 0
        self.onesf = P.sb("onesf", [128, 128], F32)
        self.epsT = P.sb("epsT", [128, 1], F32)
        P.op("pool", lambda e: e.memset(self.onesf[:], 1.0), writes=["onesf"])
        P.op("pool", lambda e: e.memset(self.epsT[:], EPS), writes=["epsT"])
        self.rstd = [P.sb(f"rstd{i}", [128, 512], F32) for i in range(2)]
        self.wst = [P.sb(f"wst{i}", [128, 16, 128], F32) for i in range(2)]
        self.wbf = [P.sb(f"wbf{i}", [128, 16, 128], BF16) for i in range(2)]
        self.w_rr = 0

    def scratch(self):
        i = self.scr_rr % len(self.scr)
        self.scr_rr += 1
        return self.scr[i], f"scr{i}"

    def bank(self, lo=0, hi=7):
        n = hi - lo
        i = lo + self.bank_rr % n
        self.bank_rr += 1
        return self.pb[i], f"pb{i}"

    def weight_block(self, src_ap):
        P = self.P
        i = self.w_rr % 2
        self.w_rr += 1
        st, bf = self.wst[i], self.wbf[i]
        P.dma(st[:], src_ap.rearrange("(k p) n -> p k n", p=128), writes=[f"wst{i}"], key=f"wst{i}")
        P.op("pool", lambda e: e.tensor_copy(out=bf[:], in_=st[:]), reads=[f"wst{i}"], writes=[f"wbf{i}"])
        return bf, f"wbf{i}"


def rmsnorm_T(C, xT, xres, gT, outT, outres, nfeat_chunks=16, out_dt_note=""):
    P = C.P
    inv = 1.0 / (128 * nfeat_chunks)
    for tb in range(2):
        sl = slice(tb * 512, (tb + 1) * 512)
        for dc in range(nfeat_chunks):
            sq, sqr = C.scratch()
            P.op("act", lambda e, sq=sq, dc=dc, sl=sl: e.activation(out=sq[:], in_=xT[:, dc, sl], func=AF.Square),
                 reads=[xres(dc, tb)], writes=[sqr])
            P.op("pe", lambda e, sq=sq, dc=dc: e.matmul(C.pb[7][:], lhsT=C.onesf[:], rhs=sq[:], start=(dc == 0),
                                                        stop=(dc == nfeat_chunks - 1)),
                 reads=[sqr, "onesf"], writes=["pb7"])
        sd, sdr = C.scratch()
        P.op("act", lambda e, sd=sd: e.activation(out=sd[:], in_=C.pb[7][:], func=AF.Sqrt, bias=C.epsT[:, 0:1], scale=inv),
             reads=["pb7", "epsT"], writes=[sdr])
        P.op("dve", lambda e, sd=sd, tb=tb: e.reciprocal(out=C.rstd[tb][:], in_=sd[:]), reads=[sdr], writes=[f"rstd{tb}"])
        for dc in range(nfeat_chunks):
            P.op("dve", lambda e, dc=dc, sl=sl, tb=tb: e.scalar_tensor_tensor(
                out=outT[:, dc, sl], in0=xT[:, dc, sl], scalar=gT[:, dc:dc + 1], in1=C.rstd[tb][:],
                op0=ALU.mult, op1=ALU.mult),
                reads=[xres(dc, tb), f"rstd{tb}", "gains"], writes=[outres(dc, tb)])


def din(nc, name, shape, dt=F32):
    return nc.dram_tensor(name, list(shape), dt, kind="ExternalInput").ap()


def dout(nc, name, shape, dt=F32):
    return nc.dram_tensor(name, list(shape), dt, kind="ExternalOutput").ap()


def build_A():
    nc = bass.Bass("TRN2", target_bir_lowering=False)
    P = Prog(nc)
    xT_d = din(nc, "xT", [D, TOK])
    w_in = din(nc, "w_in", [D, 6144])
    g_d = din(nc, "g_attn", [128, 16])
    rope_d = din(nc, "rope", [4, 128, TOK])
    perm_d = din(nc, "perm", [128, 128])
    qT_o = dout(nc, "qT", [2048, TOK], BF16)
    kdT_o = dout(nc, "kdT", [1024, TOK], BF16)
    vd_o = dout(nc, "vd", [1024, 1024], BF16)
    kn_o = dout(nc, "kn", [TOK, 1024], BF16)
    vn_o = dout(nc, "vn", [TOK, 1024], BF16)

    C = Ctx(P, nc)
    xT = P.sb("xTs", [128, 16, TOK], F32)
    hT = P.sb("hTs", [128, 16, TOK], BF16)
    gT = P.sb("gT", [128, 16], F32)
    rope = P.sb("ropes", [128, 4, TOK], F32)
    perm = P.sb("perms", [128, 128], F32)
    stg = [P.sb(f"stg{i}", [128, TOK], BF16) for i in range(3)]
    xres = lambda dc, tb: f"xT{dc}_{tb}"
    hres = lambda dc, tb: f"hT{dc}_{tb}"

    for dc in range(16):
        P.dma(xT[:, dc, :], xT_d[dc * 128:(dc + 1) * 128, :], writes=[xres(dc, 0), xres(dc, 1)], key="xin", grp=0)
    P.dma(gT[:], g_d, writes=["gains"], key="c0")
    P.dma(rope[:], rope_d.rearrange("f p t -> p f t"), writes=["rope"], key="c1")
    P.dma(perm[:], perm_d, writes=["perm"], key="c2")

    rmsnorm_T(C, xT, xres, gT, hT, hres)

    stg_rr = 0
    for blk in range(48):
        kind, h = blk // 8, blk % 8
        wb, wres = C.weight_block(w_in[:, blk * 128:(blk + 1) * 128])
        st = stg[stg_rr % 3]
        sres = f"stg{stg_rr % 3}"
        stg_rr += 1
        if kind in (0, 1, 3):
            for tb in range(2):
                sl = slice(tb * 512, (tb + 1) * 512)
                pbk, pres = C.bank(0, 5)
                for dc in range(16):
                    P.op("pe", lambda e, pbk=pbk, wb=wb, dc=dc, sl=sl: e.matmul(
                        pbk[:], lhsT=wb[:, dc, :], rhs=hT[:, dc, sl], start=(dc == 0), stop=(dc == 15)),
                        reads=[wres, hres(dc, tb)], writes=[pres])
                if kind == 3:
                    P.op("act", lambda e, st=st, pbk=pbk, sl=sl: e.mul(st[:, sl], pbk[:], 128.0 ** -0.5),
                         reads=[pres], writes=[sres])
                else:
                    ci, si = (0, 1) if kind == 0 else (2, 3)
                    xs, xsr = C.scratch()
                    P.op("act", lambda e, xs=xs, pbk=pbk: e.activation(out=xs[:], in_=pbk[:], func=AF.Copy),
                         reads=[pres], writes=[xsr])
                    p2 = C.pb[5 + (C.bank_rr % 2)]
                    p2r = f"pb{5 + (C.bank_rr % 2)}"
                    P.op("pe", lambda e, p2=p2, xs=xs: e.matmul(p2[:], lhsT=perm[:], rhs=xs[:], start=True, stop=True),
                         reads=["perm", xsr], writes=[p2r])
                    t1, t1r = C.scratch()
                    t2, t2r = C.scratch()
                    P.op("dve", lambda e, t1=t1, xs=xs, ci=ci, sl=sl: e.tensor_tensor(
                        out=t1[:], in0=xs[:], in1=rope[:, ci, sl], op=ALU.mult), reads=[xsr, "rope"], writes=[t1r])
                    P.op("dve", lambda e, t2=t2, p2=p2, si=si, sl=sl: e.tensor_tensor(
                        out=t2[:], in0=p2[:], in1=rope[:, si, sl], op=ALU.mult), reads=[p2r, "rope"], writes=[t2r])
                    P.op("dve", lambda e, st=st, t1=t1, t2=t2, sl=sl: e.tensor_tensor(
                        out=st[:, sl], in0=t1[:], in1=t2[:], op=ALU.add), reads=[t1r, t2r], writes=[sres])
            if kind == 0:
                dst = qT_o[h * 128:(h + 1) * 128, :]
            elif kind == 3:
                dst = qT_o[1024 + h * 128:1024 + (h + 1) * 128, :]
            else:
                dst = kdT_o[h * 128:(h + 1) * 128, :]
            P.finals.append(P.dma(dst, st[:], reads=[sres], key=f"so{(stg_rr - 1) % 3}"))
        else:
            for half in range(2):
                pbk, pres = C.bank(0, 5)
                for tti in range(4):
                    tt = half * 4 + tti
                    for dc in range(16):
                        P.op("pe", lambda e, pbk=pbk, wb=wb, dc=dc, tt=tt, tti=tti: e.matmul(
                            pbk[:, tti * 128:(tti + 1) * 128], lhsT=hT[:, dc, tt * 128:(tt + 1) * 128], rhs=wb[:, dc, :],
                            start=(dc == 0), stop=(dc == 15)),
                            reads=[wres, hres(dc, tt // 4)], writes=[pres])
                P.op("act", lambda e, st=st, pbk=pbk, half=half: e.activation(
                    out=st[:, half * 512:(half + 1) * 512], in_=pbk[:], func=AF.Copy), reads=[pres], writes=[sres])
            if kind == 2:
                P.finals.append(P.dma(vd_o[h * 128:(h + 1) * 128, :], st[:], reads=[sres], key=f"so{(stg_rr - 1) % 3}"))
            else:
                o = kn_o if kind == 4 else vn_o
                dst = o.rearrange("(t p) c -> p t c", p=128)[:, :, h * 128:(h + 1) * 128]
                P.finals.append(P.dma(dst, st[:].rearrange("p (t e) -> p t e", e=128), reads=[sres],
                                      key=f"so{(stg_rr - 1) % 3}"))
    P.emit()
    return nc


def build_B(final=False, debug=False):
    nc = bass.Bass("TRN2", target_bir_lowering=False)
    P = Prog(nc)
    xT_d = din(nc, "xT", [D, TOK])
    qT_d = din(nc, "qT", [2048, TOK], BF16)
    kdT_d = din(nc, "kdT_all", [8192, TOK], BF16)
    vd_d = din(nc, "vd_all", [8192, 1024], BF16)
    knw_d = din(nc, "knw", [1536, 1024], BF16)
    vnw_d = din(nc, "vnw", [1536, 1024], BF16)
    w_out = din(nc, "w_out", [D, D])
    w_mi = din(nc, "w_mlp_in", [D, 8192])
    w_mo = din(nc, "w_mlp_out", [8192, D])
    gm_d = din(nc, "g_mlp", [128, 16])
    gf_d = din(nc, "g_final", [128, 16])
    sub_d = din(nc, "subln", [128, 1])
    nag_d = din(nc, "na_g", [128, 8])
    lam_d = din(nc, "lam", [64, 4])
    cst_d = din(nc, "consts", [128, 2])
    natab_d = din(nc, "natab", [8, 128, NBLK * 64])
    rowsel_d = din(nc, "rowsel", [32, 1536], BF16)
    mq_d = din(nc, "mq", [32, TOK], BF16)
    ident_d = din(nc, "ident", [128, 128], BF16)
    xT_o = dout(nc, "xT_out", [D, TOK])

    C = Ctx(P, nc)
    xT = P.sb("xTs", [128, 16, TOK], F32)
    qT = P.sb("qTs", [128, 16, TOK], BF16)
    mixT = P.sb("mixTs", [128, 16, TOK], BF16)
    gm = P.sb("gm", [128, 16], F32)
    gf = P.sb("gf", [128, 16], F32)
    sub = P.sb("sub", [128, 1], F32)
    gsub = P.sb("gsub", [128, 1], F32)
    nag = P.sb("nag", [128, 8], F32)
    lamt = P.sb("lamt", [64, 4], F32)
    lprod = P.sb("lprod", [64, 2], F32)
    lexp = P.sb("lexp", [128, 2], F32)
    neglam = P.sb("neglam", [128, 1], F32)
    cst = P.sb("cst", [128, 2], F32)
    natab = P.sb("natab_s", [128, NBLK * 64], F32)
    rowsel = P.sb("rowsel_s", [32, 1536], BF16)
    mq = P.sb("mq_s", [32, TOK], BF16)
    ident = P.sb("ident_s", [128, 128], BF16)
    onesb = P.sb("onesb", [128, 128], BF16)
    Ks = [P.sb(f"Ks{i}", [128, 2048], BF16) for i in range(2)]
    Vs = [P.sb(f"Vs{i}", [128, 16, 128], BF16) for i in range(2)]
    PT = [P.sb(f"PT{i}", [128, 512], BF16) for i in range(3)]
    sadd = [P.sb(f"sadd{i}", [128, 512], F32) for i in range(2)]

    xres = lambda dc, tb: f"xT{dc}_{tb}"
    qres = lambda c, tb: f"qT{c}_{tb}"
    mres = lambda c, tb: f"mT{c}_{tb}"

    for dc in range(16):
        P.dma(xT[:, dc, :], xT_d[dc * 128:(dc + 1) * 128, :], writes=[xres(dc, 0), xres(dc, 1)], key="xin", grp=0)
        P.dma(qT[:, dc, :], qT_d[dc * 128:(dc + 1) * 128, :], writes=[qres(dc, 0), qres(dc, 1)], key="qin", grp=0)
    P.dma(gm[:], gm_d, writes=["gains"], key="c0")
    P.dma(gf[:], gf_d, writes=["gf"], key="c1")
    P.dma(sub[:], sub_d, writes=["sub"], key="c2")
    P.dma(nag[:], nag_d, writes=["nag"], key="c3")
    P.dma(lamt[:], lam_d, writes=["lamt"], key="c4")
    P.dma(cst[:], cst_d, writes=["cst"], key="c5")
    P.dma(rowsel[:], rowsel_d, writes=["rowsel"], key="c6")
    P.dma(mq[:], mq_d, writes=["mq"], key="c7")
    P.dma(ident[:], ident_d, writes=["ident"], key="c8")
    P.op("pool", lambda e: e.memset(onesb[:], 1.0), writes=["onesb"])

    P.op("dve", lambda e: e.tensor_tensor(out=lprod[:, 0:1], in0=lamt[:, 0:1], in1=lamt[:, 1:2], op=ALU.mult),
         reads=["lamt"], writes=["lprod"])
    P.op("dve", lambda e: e.tensor_tensor(out=lprod[:, 1:2], in0=lamt[:, 2:3], in1=lamt[:, 3:4], op=ALU.mult),
         reads=["lamt"], writes=["lprod"])
    P.op("pe", lambda e: e.matmul(C.pb[7][:, 0:2], lhsT=C.onesf[0:64, :], rhs=lprod[:], start=True, stop=True),
         reads=["lprod", "onesf"], writes=["pb7"])
    P.op("act", lambda e: e.activation(out=lexp[:], in_=C.pb[7][:, 0:2], func=AF.Exp), reads=["pb7"], writes=["lexp"])
    P.op("dve", lambda e: e.tensor_tensor(out=neglam[:], in0=lexp[:, 1:2], in1=lexp[:, 0:1], op=ALU.subtract),
         reads=["lexp"], writes=["neglam"])
    P.op("dve", lambda e: e.tensor_tensor(out=neglam[:], in0=neglam[:], in1=cst[:, 0:1], op=ALU.subtract),
         reads=["neglam", "cst"], writes=["neglam"])
    P.op("dve", lambda e: e.tensor_tensor(out=gsub[:], in0=sub[:], in1=cst[:, 1:2], op=ALU.mult),
         reads=["sub", "cst"], writes=["gsub"])

    sbank = [0, 1, 2]
    s_rr = [0]
    pt_rr = [0]

    def attn_steps(steps, accO, accS):
        LA = 2
        n = len(steps)
        info = [None] * n
        for s in range(n + LA):
            if s < n:
                stp = steps[s]
                if stp.get("pre") is not None:
                    stp["pre"]()
                bi = sbank[s_rr[0] % 3]
                s_rr[0] += 1
                pbk, pres = C.pb[bi], f"pb{bi}"
                nq = len(stp["qk"])
                for qi, (lhsT, rhs, rds) in enumerate(stp["qk"]):
                    P.op("pe", lambda e, pbk=pbk, lhsT=lhsT, rhs=rhs, qi=qi, nq=nq: e.matmul(
                        pbk[:], lhsT=lhsT, rhs=rhs, start=(qi == 0), stop=(qi == nq - 1)), reads=rds, writes=[pres])
                pi = pt_rr[0] % 3
                pt_rr[0] += 1
                pt, ptr = PT[pi], f"PT{pi}"
                if stp.get("tab") is not None:
                    sa = sadd[s % 2]
                    sar = f"sadd{s % 2}"
                    tab = stp["tab"]
                    P.op("dve", lambda e, sa=sa, pbk=pbk, tab=tab: e.tensor_tensor(out=sa[:], in0=pbk[:], in1=tab, op=ALU.add),
                         reads=[pres, "natab"], writes=[sar])
                    P.op("act", lambda e, pt=pt, sa=sa: e.activation(out=pt[:], in_=sa[:], func=AF.Exp),
                         reads=[sar], writes=[ptr])
                else:
                    P.op("act", lambda e, pt=pt, pbk=pbk: e.activation(out=pt[:], in_=pbk[:], func=AF.Exp),
                         reads=[pres], writes=[ptr])
                info[s] = (pt, ptr)
            t = s - LA
            if t >= 0:
                stp = steps[t]
                pt, ptr = info[t]
                m = stp["m"]
                vap = stp["v"][0]
                vres = list(stp["v"][1:])
                P.op("pe", lambda e, m=m, vap=vap, pt=pt, stp=stp: e.matmul(
                    C.pb[accO[m]][:], lhsT=vap, rhs=pt[:], start=stp["first"], stop=stp["last"]),
                    reads=vres + [ptr], writes=[f"pb{accO[m]}"])
                P.op("pe", lambda e, m=m, pt=pt, stp=stp: e.matmul(
                    C.pb[accS[m]][:], lhsT=onesb[:], rhs=pt[:], start=stp["first"], stop=stp["last"]),
                    reads=["onesb", ptr], writes=[f"pb{accS[m]}"])

    knw_v = knw_d.rearrange("(j p) c -> p j c", p=128)
    vnw_v = vnw_d.rearrange("(j p) c -> p j c", p=128)
    pb7b = C.pb[7][:].bitcast(BF16)
    for h in range(8):
        knw, knwr = Vs[0], "Vs0"
        vnw, vnwr = Vs[1], "Vs1"
        knT, knTr = Ks[h % 2], f"Ks{h % 2}"
        P.dma(knw[:, 0:12, :], knw_v[:, :, h * 128:(h + 1) * 128], writes=["Vs0_0", "Vs0_1"], key="Vs0_0")
        P.dma(vnw[:, 0:12, :], vnw_v[:, :, h * 128:(h + 1) * 128], writes=["Vs1_0", "Vs1_1"], key="Vs1_0")
        P.dma(natab[:], natab_d[h], writes=["natab"], key="natab")
        for jg in range(3):
            for ji in range(4):
                j = jg * 4 + ji
                P.op("pe", lambda e, ji=ji, j=j, knw=knw: e.transpose(
                    out=pb7b[:, ji * 128:(ji + 1) * 128], in_=knw[:, j, :], identity=ident[:]),
                    reads=["Vs0_0", "Vs0_1", "ident"], writes=["pb7"])
            P.op("act", lambda e, jg=jg, knT=knT: e.activation(
                out=knT[:, jg * 512:(jg + 1) * 512], in_=pb7b[:, 0:512], func=AF.Copy), reads=["pb7"], writes=[knTr + "_0", knTr + "_1"])
        for qb in range(2):
            sl = slice(qb * 512, (qb + 1) * 512)
            js = list(range(0, 10)) if qb == 0 else list(range(2, 12))
            steps = []
            for idx, j in enumerate(js):
                b0 = 8 * qb - 2 * j + B0
                steps.append({
                    "qk": [(knT[:, j * 128:(j + 1) * 128], qT[:, 8 + h, sl], [knTr + "_0", knTr + "_1", qres(8 + h, qb)]),
                           (rowsel[:, j * 128:(j + 1) * 128], mq[:, sl], ["rowsel", "mq"])],
                    "tab": natab[:, b0 * 64:(b0 + 8) * 64],
                    "v": (vnw[:, j, :], "Vs1_0", "Vs1_1"), "m": 0, "first": idx == 0, "last": idx == len(js) - 1})
            attn_steps(steps, [3], [5])
            rec, recr = C.scratch()
            o32, o32r = C.scratch()
            sq, sqr = C.scratch()
            P.op("dve", lambda e, rec=rec: e.reciprocal(out=rec[:], in_=C.pb[5][:]), reads=["pb5"], writes=[recr])
            P.op("dve", lambda e, rec=rec, o32=o32: e.tensor_tensor(out=o32[:], in0=C.pb[3][:], in1=rec[:], op=ALU.mult),
                 reads=["pb3", recr], writes=[o32r])
            P.op("act", lambda e, o32=o32, h=h, sl=sl: e.activation(out=mixT[:, 8 + h, sl], in_=o32[:], func=AF.Copy),
                 reads=[o32r], writes=[mres(8 + h, qb)])
            P.op("act", lambda e, o32=o32, sq=sq: e.activation(out=sq[:], in_=o32[:], func=AF.Square),
                 reads=[o32r], writes=[sqr])
            P.op("pe", lambda e, sq=sq: e.matmul(C.pb[7][:], lhsT=C.onesf[:], rhs=sq[:], start=True, stop=True),
                 reads=[sqr, "onesf"], writes=["pb7"])
            if h == 0:
                P.op("dve", lambda e, qb=qb: e.tensor_copy(out=C.rstd[qb][:], in_=C.pb[7][:]), reads=["pb7"], writes=[f"rstd{qb}"])
            else:
                P.op("dve", lambda e, qb=qb: e.tensor_tensor(out=C.rstd[qb][:], in0=C.pb[7][:], in1=C.rstd[qb][:], op=ALU.add),
                     reads=["pb7", f"rstd{qb}"], writes=[f"rstd{qb}"])
    for qb in range(2):
        sl = slice(qb * 512, (qb + 1) * 512)
        sd, sdr = C.scratch()
        P.op("act", lambda e, sd=sd, qb=qb: e.activation(out=sd[:], in_=C.rstd[qb][:], func=AF.Sqrt, bias=C.epsT[:, 0:1],
                                                         scale=1.0 / 1024.0), reads=[f"rstd{qb}", "epsT"], writes=[sdr])
        P.op("dve", lambda e, sd=sd, qb=qb: e.reciprocal(out=C.rstd[qb][:], in_=sd[:]), reads=[sdr], writes=[f"rstd{qb}"])
        for h in range(8):
            P.op("dve", lambda e, h=h, sl=sl, qb=qb: e.scalar_tensor_tensor(
                out=mixT[:, 8 + h, sl], in0=mixT[:, 8 + h, sl], scalar=nag[:, h:h + 1], in1=C.rstd[qb][:],
                op0=ALU.mult, op1=ALU.mult), reads=[mres(8 + h, qb), "nag", f"rstd{qb}"], writes=[mres(8 + h, qb)])

    chunks = [(h, qb, ch) for h in range(8) for qb in range(2) for ch in range(8)]

    def kv_slot(g):
        si = g % 4
        i, j = si // 2, si % 2
        return Ks[i], f"Ks{i}_{j}", Vs[i], f"Vs{i}_{j}", j

    def issue(g):
        if g >= len(chunks):
            return
        h_, qb_, ch_ = chunks[g]
        K, Kr, V, Vr, j = kv_slot(g)
        r0 = ch_ * 1024 + h_ * 128
        P.dma(K[:, j * 1024:(j + 1) * 1024], kdT_d[r0:r0 + 128, :], writes=[Kr], key=Kr)
        P.dma(V[:, j * 8:(j + 1) * 8, :].rearrange("p t e -> p (t e)"), vd_d[r0:r0 + 128, :], writes=[Vr], key=Vr)

    issue(0)
    issue(1)
    for h in range(8):
        for qb in range(2):
            sl = slice(qb * 512, (qb + 1) * 512)
            steps = []
            for ch in range(8):
                g = (h * 2 + qb) * 8 + ch
                K, Kr, V, Vr, j = kv_slot(g)
                for kt in range(8):
                    for m in range(2):
                        ps = slice(64 * m, 64 * m + 64)
                        steps.append({
                            "pre": (lambda g=g: issue(g + 2)) if (kt == 0 and m == 0) else None,
                            "qk": [(K[ps, j * 1024 + kt * 128:j * 1024 + (kt + 1) * 128], qT[ps, h, sl], [Kr, qres(h, qb)])],
                            "v": (V[:, j * 8 + kt, :], Vr), "m": m,
                            "first": ch == 0 and kt == 0, "last": ch == 7 and kt == 7})
            attn_steps(steps, [3, 4], [5, 6])
            if debug and h == 0 and qb == 0:
                dO = P.sb("dO", [128, 512], F32)
                dS = P.sb("dS", [128, 512], F32)
                dO_o = dout(nc, "o0_dbg", [128, 512])
                dS_o = dout(nc, "s0_dbg", [128, 512])
                P.op("dve", lambda e: e.tensor_copy(out=dO[:], in_=C.pb[3][:]), reads=["pb3"], writes=["dO"])
                P.op("dve", lambda e: e.tensor_copy(out=dS[:], in_=C.pb[5][:]), reads=["pb5"], writes=["dS"])
                P.finals.append(P.dma(dO_o, dO[:], reads=["dO"], key="dO"))
                P.finals.append(P.dma(dS_o, dS[:], reads=["dS"], key="dS"))
            rec0, rec0r = C.scratch()
            rec1, rec1r = C.scratch()
            o32, o32r = C.scratch()
            t32, t32r = C.scratch()
            P.op("dve", lambda e, rec0=rec0: e.reciprocal(out=rec0[:], in_=C.pb[5][:]), reads=["pb5"], writes=[rec0r])
            P.op("dve", lambda e, rec1=rec1: e.reciprocal(out=rec1[:], in_=C.pb[6][:]), reads=["pb6"], writes=[rec1r])
            P.op("dve", lambda e, rec0=rec0, o32=o32: e.tensor_tensor(out=o32[:], in0=C.pb[3][:], in1=rec0[:], op=ALU.mult),
                 reads=["pb3", rec0r], writes=[o32r])
            P.op("dve", lambda e, rec1=rec1, t32=t32: e.tensor_tensor(out=t32[:], in0=C.pb[4][:], in1=rec1[:], op=ALU.mult),
                 reads=["pb4", rec1r], writes=[t32r])
            P.op("dve", lambda e, o32=o32, t32=t32: e.scalar_tensor_tensor(
                out=o32[:], in0=t32[:], scalar=neglam[:, 0:1], in1=o32[:], op0=ALU.mult, op1=ALU.add),
                reads=[t32r, o32r, "neglam"], writes=[o32r])
            sq, sqr = C.scratch()
            P.op("act", lambda e, o32=o32, sq=sq: e.activation(out=sq[:], in_=o32[:], func=AF.Square), reads=[o32r], writes=[sqr])
            P.op("pe", lambda e, sq=sq: e.matmul(C.pb[7][:], lhsT=C.onesf[:], rhs=sq[:], start=True, stop=True),
                 reads=[sqr, "onesf"], writes=["pb7"])
            sd, sdr = C.scratch()
            P.op("act", lambda e, sd=sd: e.activation(out=sd[:], in_=C.pb[7][:], func=AF.Sqrt, bias=C.epsT[:, 0:1],
                                                      scale=1.0 / 128.0), reads=["pb7", "epsT"], writes=[sdr])
            P.op("dve", lambda e, sd=sd: e.reciprocal(out=sd[:], in_=sd[:]), reads=[sdr], writes=[sdr])
            P.op("dve", lambda e, o32=o32, sd=sd, h=h, sl=sl: e.scalar_tensor_tensor(
                out=mixT[:, h, sl], in0=o32[:], scalar=gsub[:, 0:1], in1=sd[:], op0=ALU.mult, op1=ALU.mult),
                reads=[o32r, sdr, "gsub"], writes=[mres(h, qb)])

    if debug:
        mix_o = dout(nc, "mix_dbg", [2048, TOK], BF16)
        for cc in range(16):
            P.finals.append(P.dma(mix_o[cc * 128:(cc + 1) * 128, :], mixT[:, cc, :], reads=[mres(cc, 0), mres(cc, 1)],
                                  key="dbg", grp=0))
    for dc in range(16):
        wb, wres = C.weight_block(w_out[:, dc * 128:(dc + 1) * 128])
        for tb in range(2):
            sl = slice(tb * 512, (tb + 1) * 512)
            pbk, pres = C.bank(0, 7)
            for cc in range(16):
                P.op("pe", lambda e, pbk=pbk, wb=wb, cc=cc, sl=sl: e.matmul(
                    pbk[:], lhsT=wb[:, cc, :], rhs=mixT[:, cc, sl], start=(cc == 0), stop=(cc == 15)),
                    reads=[wres, mres(cc, tb)], writes=[pres])
            P.op("dve", lambda e, pbk=pbk, dc=dc, sl=sl: e.tensor_tensor(
                out=xT[:, dc, sl], in0=pbk[:], in1=xT[:, dc, sl], op=ALU.add),
                reads=[pres, xres(dc, tb)], writes=[xres(dc, tb)])

    if debug:
        xa_o = dout(nc, "xattn_dbg", [D, TOK])
        for dc in range(16):
            P.finals.append(P.dma(xa_o[dc * 128:(dc + 1) * 128, :], xT[:, dc, :], reads=[xres(dc, 0), xres(dc, 1)],
                                  key="dbx", grp=0))
    h2T = mixT
    uT = qT
    rmsnorm_T(C, xT, xres, gm, h2T, mres)
    for fg in range(4):
        for fc in range(16):
            f0 = (fg * 16 + fc) * 128
            wb, wres = C.weight_block(w_mi[:, f0:f0 + 128])
            for tb in range(2):
                sl = slice(tb * 512, (tb + 1) * 512)
                pbk, pres = C.bank(0, 7)
                for dc in range(16):
                    P.op("pe", lambda e, pbk=pbk, wb=wb, dc=dc, sl=sl: e.matmul(
                        pbk[:], lhsT=wb[:, dc, :], rhs=h2T[:, dc, sl], start=(dc == 0), stop=(dc == 15)),
                        reads=[wres, mres(dc, tb)], writes=[pres])
                r32, r32r = C.scratch()
                P.op("act", lambda e, r32=r32, pbk=pbk: e.activation(out=r32[:], in_=pbk[:], func=AF.Relu),
                     reads=[pres], writes=[r32r])
                P.op("dve", lambda e, r32=r32, fc=fc, sl=sl: e.tensor_tensor(
                    out=uT[:, fc, sl], in0=r32[:], in1=r32[:], op=ALU.mult), reads=[r32r], writes=[qres(fc, tb)])
        for dc in range(16):
            wb, wres = C.weight_block(w_mo[fg * 2048:(fg + 1) * 2048, dc * 128:(dc + 1) * 128])
            for tb in range(2):
                sl = slice(tb * 512, (tb + 1) * 512)
                pbk, pres = C.bank(0, 7)
                for fc in range(16):
                    P.op("pe", lambda e, pbk=pbk, wb=wb, fc=fc, sl=sl: e.matmul(
                        pbk[:], lhsT=wb[:, fc, :], rhs=uT[:, fc, sl], start=(fc == 0), stop=(fc == 15)),
                        reads=[wres, qres(fc, tb)], writes=[pres])
                P.op("dve", lambda e, pbk=pbk, dc=dc, sl=sl: e.tensor_tensor(
                    out=xT[:, dc, sl], in0=pbk[:], in1=xT[:, dc, sl], op=ALU.add),
                    reads=[pres, xres(dc, tb)], writes=[xres(dc, tb)])

    if final:
        yT = P.sb("yT", [128, 16, TOK], F32) if False else None
        inv = 1.0 / 2048.0
        for tb in range(2):
            sl = slice(tb * 512, (tb + 1) * 512)
            for dc in range(16):
                sq, sqr = C.scratch()
                P.op("act", lambda e, sq=sq, dc=dc, sl=sl: e.activation(out=sq[:], in_=xT[:, dc, sl], func=AF.Square),
                     reads=[xres(dc, tb)], writes=[sqr])
                P.op("pe", lambda e, sq=sq, dc=dc: e.matmul(C.pb[7][:], lhsT=C.onesf[:], rhs=sq[:], start=(dc == 0), stop=(dc == 15)),
                     reads=[sqr, "onesf"], writes=["pb7"])
            sd, sdr = C.scratch()
            P.op("act", lambda e, sd=sd: e.activation(out=sd[:], in_=C.pb[7][:], func=AF.Sqrt, bias=C.epsT[:, 0:1], scale=inv),
                 reads=["pb7", "epsT"], writes=[sdr])
            P.op("dve", lambda e, sd=sd, tb=tb: e.reciprocal(out=C.rstd[tb][:], in_=sd[:]), reads=[sdr], writes=[f"rstd{tb}"])
            for dc in range(16):
                y, yr = C.scratch()
                P.op("dve", lambda e, y=y, dc=dc, sl=sl, tb=tb: e.scalar_tensor_tensor(
                    out=y[:], in0=xT[:, dc, sl], scalar=gf[:, dc:dc + 1], in1=C.rstd[tb][:], op0=ALU.mult, op1=ALU.mult),
                    reads=[xres(dc, tb), f"rstd{tb}", "gf"], writes=[yr])
                P.finals.append(P.dma(xT_o[dc * 128:(dc + 1) * 128, sl], y[:], reads=[yr], key=f"yo{yr}"))
    else:
        for dc in range(16):
            P.finals.append(P.dma(xT_o[dc * 128:(dc + 1) * 128, :], xT[:, dc, :], reads=[xres(dc, 0), xres(dc, 1)],
                                  key="xo", grp=0))
    P.emit()
    return nc


def _gains16(v):
    return np.ascontiguousarray(np.asarray(v, np.float32).reshape(-1, 128).T)


def _rope_tables(core):
    inv_freq = (1.0 / (np.float32(10000.0) ** (np.arange(0, 64, 2, dtype=np.float32) / np.float32(64)))).astype(np.float32)
    pos = (np.arange(TOK, dtype=np.float32) + np.float32(core * TOK)).astype(np.float32)
    ang = (pos[None, :] * inv_freq[:, None]).astype(np.float32)
    c = np.cos(ang).astype(np.float32)
    s = np.sin(ang).astype(np.float32)
    Cf = np.tile(c, (4, 1))
    sign = np.where((np.arange(128) % 64) < 32, -1.0, 1.0).astype(np.float32)[:, None]
    Sf = np.tile(s, (4, 1)) * sign
    return np.stack([Cf * np.float32(0.125), Sf * np.float32(0.125), Cf, Sf]).astype(np.float32)


def _perm():
    Pm = np.zeros((128, 128), np.float32)
    for m in range(128):
        k = m + 32 if (m % 64) < 32 else m - 32
        Pm[k, m] = 1.0
    return Pm


def _na_table(rpb_l):
    c = np.arange(64)
    cs = np.clip(c - 8, 0, 48)
    cp = np.arange(64)[:, None]
    valid = (cp >= cs[None, :]) & (cp < cs[None, :] + 16)
    rel = np.clip(cp - c[None, :] + 15, 0, 30)
    out = np.full((8, 128, NBLK, 64), NEG, np.float32)
    for b in range(NBLK):
        for half in range(2):
            dr = -(b - B0) - 4 + half
            if abs(dr) <= 7:
                blk = rpb_l[:, dr + 7, :][:, rel]
                blk = np.where(valid[None], blk, np.float32(NEG))
                out[:, half * 64:(half + 1) * 64, b, :] = blk
    return np.ascontiguousarray(out.reshape(8, 128, NBLK * 64))


def _rowsel():
    R = np.zeros((32, 12 * 128), np.float32)
    for j in range(12):
        R[2 * j, j * 128:j * 128 + 64] = 1.0
        R[2 * j + 1, j * 128 + 64:(j + 1) * 128] = 1.0
    return R


def _mq(core):
    M = np.zeros((32, TOK), np.float32)
    base = 16 * core - 4
    for i in range(16):
        r = 16 * core + i
        rs = min(max(r - 4, 0), 128 - 8)
        for wr in range(24):
            rp = base + wr
            ok = (rs <= rp < rs + 8)
            if not ok:
                M[wr, i * 64:(i + 1) * 64] = NEG
    return M


_NC_CACHE = {}


def _get(name):
    if name not in _NC_CACHE:
        _NC_CACHE[name] = {"A": build_A, "B": lambda: build_B(False), "BF": lambda: build_B(True)}[name]()
    return _NC_CACHE[name]


def kernel(x, attn_norm, w_in, lambda_q1, lambda_k1, lambda_q2, lambda_k2, diff_subln, na_norm, na_rpb, w_out,
           mlp_norm, w_mlp_in, w_mlp_out, final_norm):
    f = lambda a: np.asarray(a, np.float32)
    x = f(x)
    cores = list(range(NCORES))
    xT = [np.ascontiguousarray(x[0, c * TOK:(c + 1) * TOK, :].T) for c in cores]
    ropes = [_rope_tables(c) for c in cores]
    perm = _perm()
    bf = ml_dtypes.bfloat16
    ident = np.eye(128, dtype=np.float32).astype(bf)
    rowsel = _rowsel().astype(bf)
    mqs = [_mq(c).astype(bf) for c in cores]
    for l in range(DEPTH):
        lam_init = 0.8 - 0.6 * math.exp(-0.3 * l)
        ncA = _get("A")
        w_in_l = np.ascontiguousarray(f(w_in[l]))
        g_attn = _gains16(attn_norm[l])
        insA = [{"xT": xT[c], "w_in": w_in_l, "g_attn": g_attn, "rope": ropes[c], "perm": perm} for c in cores]
        rA = run_bass_kernel_spmd(ncA, insA, core_ids=cores).results
        kdT_all = np.concatenate([rA[c]["kdT"] for c in cores], axis=0)
        vd_all = np.concatenate([rA[c]["vd"] for c in cores], axis=0)
        pad = np.zeros((256, 1024), bf)
        kn_pad = np.concatenate([pad] + [rA[c]["kn"] for c in cores] + [pad], axis=0)
        vn_pad = np.concatenate([pad] + [rA[c]["vn"] for c in cores] + [pad], axis=0)
        last = (l == DEPTH - 1)
        ncB = _get("BF" if last else "B")
        natab = _na_table(f(na_rpb[l]))
        lam = np.ascontiguousarray(np.stack([f(lambda_q1[l]), f(lambda_k1[l]), f(lambda_q2[l]), f(lambda_k2[l])], axis=1))
        consts = np.zeros((128, 2), np.float32)
        consts[:, 0] = lam_init
        consts[:, 1] = 1.0 - lam_init
        common = {"kdT_all": kdT_all, "vd_all": vd_all, "w_out": np.ascontiguousarray(f(w_out[l])),
                  "w_mlp_in": np.ascontiguousarray(f(w_mlp_in[l])), "w_mlp_out": np.ascontiguousarray(f(w_mlp_out[l])),
                  "g_mlp": _gains16(mlp_norm[l]), "g_final": _gains16(final_norm),
                  "subln": np.ascontiguousarray(f(diff_subln[l]).reshape(128, 1)),
                  "na_g": _gains16(na_norm[l]), "lam": lam, "consts": consts, "natab": natab,
                  "rowsel": rowsel, "ident": ident}
        insB = []
        for c in cores:
            d = dict(common)
            d["xT"] = xT[c]
            d["qT"] = rA[c]["qT"]
            d["knw"] = np.ascontiguousarray(kn_pad[c * 1024:c * 1024 + 1536])
            d["vnw"] = np.ascontiguousarray(vn_pad[c * 1024:c * 1024 + 1536])
            d["mq"] = mqs[c]
            insB.append(d)
        rB = run_bass_kernel_spmd(ncB, insB, core_ids=cores).results
        xT = [rB[c]["xT_out"] for c in cores]
    out = np.concatenate([xT[c].T for c in cores], axis=0)[None]
    return np.ascontiguousarray(out.astype(np.float32))
```

```python
import contextlib
import math
import numpy as np
import ml_dtypes
import concourse.bass as bass
import concourse.mybir as mybir
from concourse.bass_utils import run_bass_kernel_spmd

F32 = mybir.dt.float32
BF16 = mybir.dt.bfloat16
AF = mybir.ActivationFunctionType
ALU = mybir.AluOpType
ENGS = ["pe", "act", "dve", "pool", "sp"]

NCORES = 8
D = 2048
SEQ = 8192
TOK = 1024
DEPTH = 4
EPS = 1e-6
NEG = -30000.0
NBLK = 30
B0 = 18


class _Op:
    __slots__ = ("eng", "fn", "deps", "marked", "count", "dma_key", "dma_target", "idx", "grp", "inc")


class Prog:
    def __init__(self, nc):
        self.nc = nc
        self.ops = {e: [] for e in ENGS}
        self.last_w = {}
        self.readers = {}
        self.dma_cum = {}
        self.stack = contextlib.ExitStack()
        self.finals = []

    def sb(self, name, shape, dt):
        return self.stack.enter_context(self.nc.sbuf_tensor(name, list(shape), dt))

    def ps(self, name, shape, dt=F32):
        return self.stack.enter_context(self.nc.psum_tensor(name, list(shape), dt))

    def op(self, eng, fn, reads=(), writes=(), dma_key=None, grp=None, inc=16, novalue=False):
        o = _Op()
        o.grp = grp
        o.inc = inc
        o.eng = eng
        o.fn = fn
        deps = []
        for r in reads:
            w = self.last_w.get(r)
            if w is not None:
                deps.append(w)
        for r in writes:
            w = self.last_w.get(r)
            if w is not None and not (grp is not None and w.dma_key == dma_key and w.grp == grp):
                deps.append(w)
            deps.extend(self.readers.get(r, ()))
        o.deps = deps
        o.marked = False
        o.count = 0
        o.dma_key = dma_key
        o.dma_target = 0
        if dma_key is not None:
            self.dma_cum[dma_key] = self.dma_cum.get(dma_key, 0) + inc
            o.dma_target = self.dma_cum[dma_key]
        o.idx = len(self.ops[eng])
        self.ops[eng].append(o)
        for r in reads:
            self.readers.setdefault(r, []).append(o)
        for r in writes:
            self.last_w[r] = o
            self.readers[r] = []
        return o

    def dma(self, out, in_, reads=(), writes=(), key=None, eng="sp", grp=None):
        return self.op(eng, lambda e: e.dma_start(out=out, in_=in_), reads, writes, dma_key=key, grp=grp)

    def emit(self):
        nc = self.nc
        final_waits = self.finals
        for e in ENGS:
            for o in self.ops[e]:
                for d in o.deps:
                    if d.dma_key is None and (d.eng != o.eng or o.eng != "pe"):
                        d.marked = True
        for d in final_waits:
            if d.dma_key is None:
                d.marked = True
        gmax = {}
        for e in ENGS:
            c = 0
            for o in self.ops[e]:
                if o.marked:
                    c += 1
                    o.count = c
                if o.dma_key is not None and o.grp is not None:
                    gk = (o.dma_key, o.grp)
                    gmax[gk] = max(gmax.get(gk, 0), o.dma_target)
        self._own_target = {}
        for e in ENGS:
            for o in self.ops[e]:
                if o.dma_key is not None:
                    self._own_target[id(o)] = o.dma_target
                    if o.grp is not None:
                        o.dma_target = gmax[(o.dma_key, o.grp)]
        sems = {}
        for e in ENGS:
            sems[("eng", e)] = self.stack.enter_context(nc.semaphore("s_" + e))
        for k in self.dma_cum:
            sems[("dma", k)] = self.stack.enter_context(nc.semaphore("d_" + str(k)))
        block = self.stack.enter_context(nc.Block())

        def run(ename, eng):
            waited = {}
            for o in self.ops[ename]:
                need = {}
                for d in o.deps:
                    if d.dma_key is not None:
                        k = ("dma", d.dma_key)
                        v = d.dma_target
                    elif d.eng != ename or ename != "pe":
                        k = ("eng", d.eng)
                        v = d.count
                    else:
                        continue
                    if v > need.get(k, 0):
                        need[k] = v
                for k, v in need.items():
                    if waited.get(k, 0) < v:
                        eng.wait_ge(sems[k], v)
                        waited[k] = v
                ins = o.fn(eng)
                if ins is None:
                    continue
                if o.dma_key is not None:
                    ins.then_inc(sems[("dma", o.dma_key)], o.inc)
                elif o.marked:
                    ins.then_inc(sems[("eng", ename)], 1)
            if ename == "sp":
                fin = {}
                for d in final_waits:
                    if d.dma_key is not None:
                        k, v = ("dma", d.dma_key), d.dma_target
                    else:
                        k, v = ("eng", d.eng), d.count
                    fin[k] = max(fin.get(k, 0), v)
                for k, v in fin.items():
                    eng.wait_ge(sems[k], v)

        @block.tensor
        def _(e):
            run("pe", e)

        @block.scalar
        def _(e):
            run("act", e)

        @block.vector
        def _(e):
            run("dve", e)

        @block.gpsimd
        def _(e):
            run("pool", e)

        @block.sync
        def _(e):
            run("sp", e)

        self.stack.close()


class Ctx:
    def __init__(self, P, nc, n_scratch=5, with_bf_consts=True):
        self.P = P
        self.nc = nc
        self.pb = [P.ps(f"pb{i}", [128, 512]) for i in range(8)]
        self.bank_rr = 0
        self.scr = [P.sb(f"scr{i}", [128, 512], F32) for i in range(n_scratch)]
        self.scr_rr = 0
        self.onesf = P.sb("onesf", [128, 128], F32)
        self.epsT = P.sb("epsT", [128, 1], F32)
        P.op("pool", lambda e: e.memset(self.onesf[:], 1.0), writes=["onesf"])
        P.op("pool", lambda e: e.memset(self.epsT[:], EPS), writes=["epsT"])
        self.rstd = [P.sb(f"rstd{i}", [128, 512], F32) for i in range(2)]
        self.wst = [P.sb(f"wst{i}", [128, 16, 128], F32) for i in range(2)]
        self.wbf = [P.sb(f"wbf{i}", [128, 16, 128], BF16) for i in range(2)]
        self.w_rr = 0

    def scratch(self):
        i = self.scr_rr % len(self.scr)
        self.scr_rr += 1
        return self.scr[i], f"scr{i}"

    def bank(self, lo=0, hi=7):
        n = hi - lo
        i = lo + self.bank_rr % n
        self.bank_rr += 1
        return self.pb[i], f"pb{i}"

    def set_weights(self, aps):
        self.wlist = aps
        self.w_next = 0
        self.w_issued = 0

    def _issue_w(self, b):
        if b >= len(self.wlist) or b < self.w_issued:
            return
        i = b % 2
        self.P.dma(self.wst[i][:], self.wlist[b].rearrange("(k p) n -> p k n", p=128), writes=[f"wst{i}"], key=f"wst{i}")
        self.w_issued = b + 1

    def weight_block(self, check_ap=None):
        P = self.P
        b = self.w_next
        self.w_next += 1
        if b == 0:
            self._issue_w(0)
            self._issue_w(1)
        i = b % 2
        st, bf = self.wst[i], self.wbf[i]
        P.op("pool", lambda e: e.tensor_copy(out=bf[:], in_=st[:]), reads=[f"wst{i}"], writes=[f"wbf{i}"])
        self._issue_w(b + 2)
        return bf, f"wbf{i}"


def rmsnorm_T(C, xT, xres, gT, outT, outres, nfeat_chunks=16, gres="gains"):
    P = C.P
    inv = 1.0 / (128 * nfeat_chunks)
    for tb in range(2):
        sl = slice(tb * 512, (tb + 1) * 512)
        for dc in range(nfeat_chunks):
            sq, sqr = C.scratch()
            P.op("act", lambda e, sq=sq, dc=dc, sl=sl: e.activation(out=sq[:], in_=xT[:, dc, sl], func=AF.Square),
                 reads=[xres(dc, tb)], writes=[sqr])
            P.op("pe", lambda e, sq=sq, dc=dc: e.matmul(C.pb[7][:], lhsT=C.onesf[:], rhs=sq[:], start=(dc == 0),
                                                        stop=(dc == nfeat_chunks - 1)),
                 reads=[sqr, "onesf"], writes=["pb7"])
        sd, sdr = C.scratch()
        P.op("act", lambda e, sd=sd: e.activation(out=sd[:], in_=C.pb[7][:], func=AF.Sqrt, bias=C.epsT[:, 0:1], scale=inv),
             reads=["pb7", "epsT"], writes=[sdr])
        P.op("dve", lambda e, sd=sd, tb=tb: e.reciprocal(out=C.rstd[tb][:], in_=sd[:]), reads=[sdr], writes=[f"rstd{tb}"])
        for dc in range(nfeat_chunks):
            P.op("dve", lambda e, dc=dc, sl=sl, tb=tb: e.scalar_tensor_tensor(
                out=outT[:, dc, sl], in0=xT[:, dc, sl], scalar=gT[:, dc:dc + 1], in1=C.rstd[tb][:],
                op0=ALU.mult, op1=ALU.mult),
                reads=[xres(dc, tb), f"rstd{tb}", gres], writes=[outres(dc, tb)])


def din(nc, name, shape, dt=F32):
    return nc.dram_tensor(name, list(shape), dt, kind="ExternalInput").ap()


def dout(nc, name, shape, dt=F32):
    return nc.dram_tensor(name, list(shape), dt, kind="ExternalOutput").ap()


def build_fused(n_layers=DEPTH, debug=False):
    L = n_layers
    nc = bass.Bass("TRN2", target_bir_lowering=False)
    P = Prog(nc)
    xT_d = din(nc, "xT", [D, TOK])
    w_in = din(nc, "w_in", [L, D, 6144])
    w_out = din(nc, "w_out", [L, D, D])
    w_mi = din(nc, "w_mlp_in", [L, D, 8192])
    w_mo = din(nc, "w_mlp_out", [L, 8192, D])
    gA_d = din(nc, "g_attn", [128, L * 16])
    gM_d = din(nc, "g_mlp", [128, L * 16])
    gF_d = din(nc, "g_final", [128, 16])
    sub_d = din(nc, "subln", [128, L])
    nag_d = din(nc, "na_g", [128, L * 8])
    lam_d = din(nc, "lam", [64, L * 4])
    cst_d = din(nc, "consts", [128, L * 2])
    rope_d = din(nc, "rope", [2, 128, TOK])
    perm_d = din(nc, "perm", [128, 128])
    natab_d = din(nc, "natab", [L * 8, 128, NBLK * 64])
    rowsel_d = din(nc, "rowsel", [32, 1536], BF16)
    mq_d = din(nc, "mq", [32, TOK], BF16)
    ident_d = din(nc, "ident", [128, 128], BF16)
    selI_d = din(nc, "selI", [128, 16 * 128], BF16)
    y_o = dout(nc, "yT", [D, TOK])
    kd_loc = nc.dram_tensor("kd_loc", [1024, TOK], BF16).ap()
    vd_loc = nc.dram_tensor("vd_loc", [1024, 1024], BF16).ap()
    kn_loc = nc.dram_tensor("kn_loc", [TOK, 1024], BF16).ap()
    vn_loc = nc.dram_tensor("vn_loc", [TOK, 1024], BF16).ap()
    kd_all = [nc.dram_tensor(f"kd_all{i}", [8192, TOK], BF16).ap() for i in range(2)]
    vd_all = [nc.dram_tensor(f"vd_all{i}", [8192, 1024], BF16).ap() for i in range(2)]
    hal_loc = nc.dram_tensor("hal_loc", [1024, 1024], BF16).ap()
    hal_all = [nc.dram_tensor(f"hal_all{i}", [8192, 1024], BF16).ap() for i in range(2)]
    hal_sel = nc.dram_tensor("hal_sel", [1024, 1024], BF16).ap()

    C = Ctx(P, nc)
    wl = []
    for l_ in range(L):
        wl += [w_in[l_, :, b * 128:(b + 1) * 128] for b in range(48)]
        wl += [w_out[l_, :, b * 128:(b + 1) * 128] for b in range(16)]
        for fg_ in range(4):
            wl += [w_mi[l_, :, (fg_ * 16 + fc_) * 128:(fg_ * 16 + fc_ + 1) * 128] for fc_ in range(16)]
            wl += [w_mo[l_, fg_ * 2048:(fg_ + 1) * 2048, b * 128:(b + 1) * 128] for b in range(16)]
    C.set_weights(wl)
    xT = P.sb("xTs", [128, 16, TOK], F32)
    aT = P.sb("aTs", [128, 16, TOK], BF16)
    qT = P.sb("qTs", [128, 16, TOK], BF16)
    ropenat = P.sb("ropenat", [128, 2048], F32)
    rope = ropenat[:].rearrange("p (f t) -> p f t", f=2)
    natab = ropenat
    Ks = [P.sb(f"Ks{i}", [128, 2048], BF16) for i in range(2)]
    Vs = [P.sb(f"Vs{i}", [128, 16, 128], BF16) for i in range(2)]
    PT = [P.sb(f"PT{i}", [128, 512], BF16) for i in range(3)]
    gA = P.sb("gA", [128, L * 16], F32)
    gM = P.sb("gM", [128, L * 16], F32)
    gF = P.sb("gF", [128, 16], F32)
    sub = P.sb("sub", [128, L], F32)
    nag = P.sb("nag", [128, L * 8], F32)
    lamt = P.sb("lamt", [64, L * 4], F32)
    cst = P.sb("cst", [128, L * 2], F32)
    gsub = P.sb("gsub", [128, 1], F32)
    lprod = P.sb("lprod", [64, 2], F32)
    lexp = P.sb("lexp", [128, 2], F32)
    neglam = P.sb("neglam", [128, 1], F32)
    perm = P.sb("perms", [128, 128], F32)
    rowsel = P.sb("rowsel_s", [32, 1536], BF16)
    mq = P.sb("mq_s", [32, TOK], BF16)
    ident = P.sb("ident_s", [128, 128], BF16)
    onesb = P.sb("onesb", [128, 128], BF16)
    selI = P.sb("selI_s", [128, 16, 128], BF16)

    xres = lambda dc, tb: f"xT{dc}_{tb}"
    qres = lambda c, tb: f"qT{c}_{tb}"
    ares = lambda c, tb: f"aT{c}_{tb}"

    for dc in range(16):
        P.dma(xT[:, dc, :], xT_d[dc * 128:(dc + 1) * 128, :], writes=[xres(dc, 0), xres(dc, 1)], key="xin", grp=0)
    small = [(gA, gA_d, "gA"), (gM, gM_d, "gM"), (gF, gF_d, "gF"), (sub, sub_d, "sub"), (nag, nag_d, "nag"),
             (lamt, lam_d, "lamt"), (cst, cst_d, "cst"), (perm, perm_d, "perm"), (rowsel, rowsel_d, "rowsel"),
             (mq, mq_d, "mq"), (ident, ident_d, "ident")]
    P.dma(selI[:].rearrange("p r c -> p (r c)"), selI_d, writes=["selI"], key="csel")
    for i, (t, dsrc, r) in enumerate(small):
        P.dma(t[:], dsrc, writes=[r], key=f"c{i}")
    P.op("pool", lambda e: e.memset(onesb[:], 1.0), writes=["onesb"])
    sbank = [0, 1, 2]
    s_rr = [0]
    pt_rr = [0]

    def attn_steps(steps, accO, accS):
        LA = 2
        n = len(steps)
        info = [None] * n
        for s in range(n + LA):
            if s < n:
                stp = steps[s]
                if stp.get("pre") is not None:
                    stp["pre"]()
                bi = sbank[s_rr[0] % 3]
                s_rr[0] += 1
                pbk, pres = C.pb[bi], f"pb{bi}"
                nq = len(stp["qk"])
                for qi, (lhsT, rhs, rds) in enumerate(stp["qk"]):
                    P.op("pe", lambda e, pbk=pbk, lhsT=lhsT, rhs=rhs, qi=qi, nq=nq: e.matmul(
                        pbk[:], lhsT=lhsT, rhs=rhs, start=(qi == 0), stop=(qi == nq - 1)), reads=rds, writes=[pres])
                pi = pt_rr[0] % 3
                pt_rr[0] += 1
                pt, ptr = PT[pi], f"PT{pi}"
                if stp.get("tab") is not None:
                    sa, sar = C.scratch()
                    tab = stp["tab"]
                    P.op("dve", lambda e, sa=sa, pbk=pbk, tab=tab: e.tensor_tensor(out=sa[:], in0=pbk[:], in1=tab, op=ALU.add),
                         reads=[pres, "ropenat"], writes=[sar])
                    P.op("act", lambda e, pt=pt, sa=sa: e.activation(out=pt[:], in_=sa[:], func=AF.Exp),
                         reads=[sar], writes=[ptr])
                else:
                    P.op("act", lambda e, pt=pt, pbk=pbk: e.activation(out=pt[:], in_=pbk[:], func=AF.Exp),
                         reads=[pres], writes=[ptr])
                info[s] = (pt, ptr)
            t = s - LA
            if t >= 0:
                stp = steps[t]
                pt, ptr = info[t]
                m = stp["m"]
                vap = stp["v"][0]
                vres = list(stp["v"][1:])
                P.op("pe", lambda e, m=m, vap=vap, pt=pt, stp=stp: e.matmul(
                    C.pb[accO[m]][:], lhsT=vap, rhs=pt[:], start=stp["first"], stop=stp["last"]),
                    reads=vres + [ptr], writes=[f"pb{accO[m]}"])
                P.op("pe", lambda e, m=m, pt=pt, stp=stp: e.matmul(
                    C.pb[accS[m]][:], lhsT=onesb[:], rhs=pt[:], start=stp["first"], stop=stp["last"]),
                    reads=["onesb", ptr], writes=[f"pb{accS[m]}"])

    stg_rr = [0]

    def stage_slot():
        si = stg_rr[0] % 4
        stg_rr[0] += 1
        i, j = si // 2, si % 2
        return Ks[i][:, j * 1024:(j + 1) * 1024], f"Ks{i}_{j}", f"so{si}"

    gl = {"g": 0}

    for l in range(L):
        par = l % 2
        P.dma(rope, rope_d.rearrange("f p t -> p f t"), writes=["ropenat"], key="ropenat")
        rmsnorm_T(C, xT, xres, gA[:, l * 16:(l + 1) * 16], aT, ares, gres="gA")
        for blk in range(48):
            kind, h = blk // 8, blk % 8
            wb, wres = C.weight_block()
            st, sres, skey = stage_slot()
            if kind in (0, 1, 3):
                for tb in range(2):
                    sl = slice(tb * 512, (tb + 1) * 512)
                    pbk, pres = C.bank(0, 5)
                    for dc in range(16):
                        P.op("pe", lambda e, pbk=pbk, wb=wb, dc=dc, sl=sl: e.matmul(
                            pbk[:], lhsT=wb[:, dc, :], rhs=aT[:, dc, sl], start=(dc == 0), stop=(dc == 15)),
                            reads=[wres, ares(dc, tb)], writes=[pres])
                    if kind == 3:
                        P.op("act", lambda e, pbk=pbk, sl=sl, h=h: e.mul(qT[:, 8 + h, sl], pbk[:], 128.0 ** -0.5),
                             reads=[pres], writes=[qres(8 + h, tb)])
                    else:
                        xs, xsr = C.scratch()
                        if kind == 0:
                            P.op("act", lambda e, xs=xs, pbk=pbk: e.mul(xs[:], pbk[:], 0.125), reads=[pres], writes=[xsr])
                        else:
                            P.op("act", lambda e, xs=xs, pbk=pbk: e.activation(out=xs[:], in_=pbk[:], func=AF.Copy),
                                 reads=[pres], writes=[xsr])
                        p2i = 5 + (C.bank_rr % 2)
                        p2, p2r = C.pb[p2i], f"pb{p2i}"
                        P.op("pe", lambda e, p2=p2, xs=xs: e.matmul(p2[:], lhsT=perm[:], rhs=xs[:], start=True, stop=True),
                             reads=["perm", xsr], writes=[p2r])
                        t1, t1r = C.scratch()
                        t2, t2r = C.scratch()
                        P.op("dve", lambda e, t1=t1, xs=xs, sl=sl: e.tensor_tensor(
                            out=t1[:], in0=xs[:], in1=rope[:, 0, sl], op=ALU.mult), reads=[xsr, "ropenat"], writes=[t1r])
                        P.op("dve", lambda e, t2=t2, p2=p2, sl=sl: e.tensor_tensor(
                            out=t2[:], in0=p2[:], in1=rope[:, 1, sl], op=ALU.mult), reads=[p2r, "ropenat"], writes=[t2r])
                        if kind == 0:
                            P.op("dve", lambda e, t1=t1, t2=t2, sl=sl, h=h: e.tensor_tensor(
                                out=qT[:, h, sl], in0=t1[:], in1=t2[:], op=ALU.add), reads=[t1r, t2r], writes=[qres(h, tb)])
                        else:
                            P.op("dve", lambda e, st=st, t1=t1, t2=t2, sl=sl: e.tensor_tensor(
                                out=st[:, sl], in0=t1[:], in1=t2[:], op=ALU.add), reads=[t1r, t2r], writes=[sres])
                if kind == 1:
                    P.dma(kd_loc[h * 128:(h + 1) * 128, :], st, reads=[sres], writes=["kd_loc"], key=skey)
            else:
                for half in range(2):
                    pbk, pres = C.bank(0, 5)
                    for tti in range(4):
                        tt = half * 4 + tti
                        for dc in range(16):
                            P.op("pe", lambda e, pbk=pbk, wb=wb, dc=dc, tt=tt, tti=tti: e.matmul(
                                pbk[:, tti * 128:(tti + 1) * 128], lhsT=aT[:, dc, tt * 128:(tt + 1) * 128], rhs=wb[:, dc, :],
                                start=(dc == 0), stop=(dc == 15)),
                                reads=[wres, ares(dc, tt // 4)], writes=[pres])
                    P.op("act", lambda e, st=st, pbk=pbk, half=half: e.activation(
                        out=st[:, half * 512:(half + 1) * 512], in_=pbk[:], func=AF.Copy), reads=[pres], writes=[sres])
                if kind == 2:
                    P.dma(vd_loc[h * 128:(h + 1) * 128, :], st, reads=[sres], writes=["vd_loc"], key=skey)
                else:
                    o, ores = (kn_loc, "kn_loc") if kind == 4 else (vn_loc, "vn_loc")
                    dst = o.rearrange("(t p) c -> p t c", p=128)[:, :, h * 128:(h + 1) * 128]
                    P.dma(dst, st.rearrange("p (t e) -> p t e", e=128), reads=[sres], writes=[ores], key=skey)

        for i, (src, sres_, r0) in enumerate(((kn_loc, "kn_loc", 0), (kn_loc, "kn_loc", 768), (vn_loc, "vn_loc", 0), (vn_loc, "vn_loc", 768))):
            P.dma(hal_loc[i * 256:(i + 1) * 256, :], src[r0:r0 + 256, :], reads=[sres_], writes=[f"hal_loc{i}"], key="hal", grp=l)
        g_all = [list(range(NCORES))]
        hl = [f"hal_loc{i}" for i in range(4)]
        for (loc, lres, dstap, dres, ckey) in (
                (hal_loc, hl, hal_all[par], f"hal_all{par}", "agh"),
                (kd_loc, ["kd_loc"], kd_all[par], f"kd_all{par}", "agkd"),
                (vd_loc, ["vd_loc"], vd_all[par], f"vd_all{par}", "agvd")):
            P.op("pool", lambda e, loc=loc, dstap=dstap: e.collective_compute(
                "AllGather", ALU.bypass, replica_groups=g_all, ins=[loc.opt()], outs=[dstap.opt()]),
                reads=lres, writes=[dres], dma_key=ckey, inc=1)
        slots8 = [(Ks[0][:, 0:1024], "Ks0_0"), (Ks[0][:, 1024:2048], "Ks0_1"), (Ks[1][:, 0:1024], "Ks1_0"),
                  (Ks[1][:, 1024:2048], "Ks1_1"),
                  (Vs[0][:, 0:8, :].rearrange("p t e -> p (t e)"), "Vs0_0"), (Vs[0][:, 8:16, :].rearrange("p t e -> p (t e)"), "Vs0_1"),
                  (Vs[1][:, 0:8, :].rearrange("p t e -> p (t e)"), "Vs1_0"), (Vs[1][:, 8:16, :].rearrange("p t e -> p (t e)"), "Vs1_1")]
        src_off = [256, 384, 0, 128, 768, 896, 512, 640]
        sel_base = [0, 0, 8, 8, 0, 0, 8, 8]
        for ty in range(8):
            for r in range(8):
                sap, sres_ = slots8[r]
                r0 = r * 1024 + src_off[ty]
                P.dma(sap, hal_all[par][r0:r0 + 128, :], reads=[f"hal_all{par}"], writes=[sres_], key=sres_)
            for half in range(2):
                pbk, pres = C.bank(0, 5)
                for r in range(8):
                    sap, sres_ = slots8[r]
                    P.op("pe", lambda e, pbk=pbk, sap=sap, r=r, half=half, ty=ty: e.matmul(
                        pbk[:], lhsT=selI[:, sel_base[ty] + r, :], rhs=sap[:, half * 512:(half + 1) * 512],
                        start=(r == 0), stop=(r == 7)), reads=["selI", sres_], writes=[pres])
                pt, ptr = PT[half], f"PT{half}"
                P.op("act", lambda e, pt=pt, pbk=pbk: e.activation(out=pt[:], in_=pbk[:], func=AF.Copy), reads=[pres], writes=[ptr])
                P.dma(hal_sel[ty * 128:(ty + 1) * 128, half * 512:(half + 1) * 512], pt[:], reads=[ptr],
                      writes=[f"hal_sel{ty}_{half}"], key=f"hs{half}")
        hsel_res = [f"hal_sel{ty}_{half}" for ty in range(8) for half in range(2)]

        P.op("dve", lambda e, l=l: e.tensor_tensor(out=lprod[:, 0:1], in0=lamt[:, 4 * l:4 * l + 1], in1=lamt[:, 4 * l + 1:4 * l + 2],
                                                   op=ALU.mult), reads=["lamt"], writes=["lprod"])
        P.op("dve", lambda e, l=l: e.tensor_tensor(out=lprod[:, 1:2], in0=lamt[:, 4 * l + 2:4 * l + 3], in1=lamt[:, 4 * l + 3:4 * l + 4],
                                                   op=ALU.mult), reads=["lamt", "lprod"], writes=["lprod"])
        P.op("pe", lambda e: e.matmul(C.pb[7][:, 0:2], lhsT=C.onesf[0:64, :], rhs=lprod[:], start=True, stop=True),
             reads=["lprod", "onesf"], writes=["pb7"])
        P.op("act", lambda e: e.activation(out=lexp[:], in_=C.pb[7][:, 0:2], func=AF.Exp), reads=["pb7"], writes=["lexp"])
        P.op("dve", lambda e: e.tensor_tensor(out=neglam[:], in0=lexp[:, 1:2], in1=lexp[:, 0:1], op=ALU.subtract),
             reads=["lexp"], writes=["neglam"])
        P.op("dve", lambda e, l=l: e.tensor_tensor(out=neglam[:], in0=neglam[:], in1=cst[:, 2 * l:2 * l + 1], op=ALU.subtract),
             reads=["neglam", "cst"], writes=["neglam"])
        P.op("dve", lambda e, l=l: e.tensor_tensor(out=gsub[:], in0=sub[:, l:l + 1], in1=cst[:, 2 * l + 1:2 * l + 2], op=ALU.mult),
             reads=["sub", "cst"], writes=["gsub"])

        pb7b = C.pb[7][:].bitcast(BF16)
        for h in range(8):
            knw, vnw = Vs[0], Vs[1]
            knT, knTr = Ks[h % 2], f"Ks{h % 2}"

            hs = slice(h * 128, (h + 1) * 128)
            for (t0, nt, ksrc, vsrc, rds) in (
                    (0, 2, hal_sel[0:256, hs], hal_sel[512:768, hs], hsel_res),
                    (2, 8, kn_loc[:, hs], vn_loc[:, hs], ["kn_loc", "vn_loc"]),
                    (10, 2, hal_sel[256:512, hs], hal_sel[768:1024, hs], hsel_res)):
                P.dma(knw[:, t0:t0 + nt, :], ksrc.rearrange("(j p) c -> p j c", p=128),
                      reads=rds, writes=["Vs0_0", "Vs0_1"], key="Vs0_0", grp=("na", l, h))
                P.dma(vnw[:, t0:t0 + nt, :], vsrc.rearrange("(j p) c -> p j c", p=128),
                      reads=rds, writes=["Vs1_0", "Vs1_1"], key="Vs1_0", grp=("na", l, h))
            P.dma(natab[:, 0:NBLK * 64], natab_d[l * 8 + h], writes=["ropenat"], key="ropenat")
            for jg in range(3):
                for ji in range(4):
                    j = jg * 4 + ji
                    P.op("pe", lambda e, ji=ji, j=j, knw=knw: e.transpose(
                        out=pb7b[:, ji * 128:(ji + 1) * 128], in_=knw[:, j, :], identity=ident[:]),
                        reads=["Vs0_0", "Vs0_1", "ident"], writes=["pb7"])
                P.op("act", lambda e, jg=jg, knT=knT: e.activation(
                    out=knT[:, jg * 512:(jg + 1) * 512], in_=pb7b[:, 0:512], func=AF.Copy),
                    reads=["pb7"], writes=[knTr + "_0", knTr + "_1"])
            for qb in range(2):
                sl = slice(qb * 512, (qb + 1) * 512)
                js = list(range(0, 10)) if qb == 0 else list(range(2, 12))
                steps = []
                for idx, j in enumerate(js):
                    b0 = 8 * qb - 2 * j + B0
                    steps.append({
                        "qk": [(knT[:, j * 128:(j + 1) * 128], qT[:, 8 + h, sl], [knTr + "_0", knTr + "_1", qres(8 + h, qb)]),
                               (rowsel[:, j * 128:(j + 1) * 128], mq[:, sl], ["rowsel", "mq"])],
                        "tab": natab[:, b0 * 64:(b0 + 8) * 64],
                        "v": (vnw[:, j, :], "Vs1_0", "Vs1_1"), "m": 0, "first": idx == 0, "last": idx == len(js) - 1})
                attn_steps(steps, [3], [5])
                rec, recr = C.scratch()
                o32, o32r = C.scratch()
                sq, sqr = C.scratch()
                P.op("dve", lambda e, rec=rec: e.reciprocal(out=rec[:], in_=C.pb[5][:]), reads=["pb5"], writes=[recr])
                P.op("dve", lambda e, rec=rec, o32=o32: e.tensor_tensor(out=o32[:], in0=C.pb[3][:], in1=rec[:], op=ALU.mult),
                     reads=["pb3", recr], writes=[o32r])
                P.op("act", lambda e, o32=o32, h=h, sl=sl: e.activation(out=aT[:, 8 + h, sl], in_=o32[:], func=AF.Copy),
                     reads=[o32r], writes=[ares(8 + h, qb)])
                P.op("act", lambda e, o32=o32, sq=sq: e.activation(out=sq[:], in_=o32[:], func=AF.Square),
                     reads=[o32r], writes=[sqr])
                P.op("pe", lambda e, sq=sq: e.matmul(C.pb[7][:], lhsT=C.onesf[:], rhs=sq[:], start=True, stop=True),
                     reads=[sqr, "onesf"], writes=["pb7"])
                if h == 0:
                    P.op("dve", lambda e, qb=qb: e.tensor_copy(out=C.rstd[qb][:], in_=C.pb[7][:]), reads=["pb7"], writes=[f"rstd{qb}"])
                else:
                    P.op("dve", lambda e, qb=qb: e.tensor_tensor(out=C.rstd[qb][:], in0=C.pb[7][:], in1=C.rstd[qb][:], op=ALU.add),
                         reads=["pb7", f"rstd{qb}"], writes=[f"rstd{qb}"])
        for qb in range(2):
            sl = slice(qb * 512, (qb + 1) * 512)
            sd, sdr = C.scratch()
            P.op("act", lambda e, sd=sd, qb=qb: e.activation(out=sd[:], in_=C.rstd[qb][:], func=AF.Sqrt, bias=C.epsT[:, 0:1],
                                                             scale=1.0 / 1024.0), reads=[f"rstd{qb}", "epsT"], writes=[sdr])
            P.op("dve", lambda e, sd=sd, qb=qb: e.reciprocal(out=C.rstd[qb][:], in_=sd[:]), reads=[sdr], writes=[f"rstd{qb}"])
            for h in range(8):
                P.op("dve", lambda e, h=h, sl=sl, qb=qb, l=l: e.scalar_tensor_tensor(
                    out=aT[:, 8 + h, sl], in0=aT[:, 8 + h, sl], scalar=nag[:, l * 8 + h:l * 8 + h + 1], in1=C.rstd[qb][:],
                    op0=ALU.mult, op1=ALU.mult), reads=[ares(8 + h, qb), "nag", f"rstd{qb}"], writes=[ares(8 + h, qb)])

        chunks = [(h, qb, ch) for h in range(8) for qb in range(2) for ch in range(8)]
        gbase = gl["g"]

        def kv_slot(g):
            si = g % 4
            i, j = si // 2, si % 2
            return Ks[i], f"Ks{i}_{j}", Vs[i], f"Vs{i}_{j}", j

        def issue(gi, par=par, gbase=gbase, chunks=chunks):
            if gi >= len(chunks):
                return
            h_, qb_, ch_ = chunks[gi]
            K, Kr, V, Vr, j = kv_slot(gbase + gi)
            r0 = ch_ * 1024 + h_ * 128
            P.dma(K[:, j * 1024:(j + 1) * 1024], kd_all[par][r0:r0 + 128, :], reads=[f"kd_all{par}"], writes=[Kr], key=Kr)
            P.dma(V[:, j * 8:(j + 1) * 8, :].rearrange("p t e -> p (t e)"), vd_all[par][r0:r0 + 128, :],
                  reads=[f"vd_all{par}"], writes=[Vr], key=Vr)

        issue(0)
        issue(1)
        for h in range(8):
            for qb in range(2):
                sl = slice(qb * 512, (qb + 1) * 512)
                steps = []
                for ch in range(8):
                    gi = (h * 2 + qb) * 8 + ch
                    K, Kr, V, Vr, j = kv_slot(gbase + gi)
                    for kt in range(8):
                        for m in range(2):
                            ps = slice(64 * m, 64 * m + 64)
                            steps.append({
                                "pre": (lambda gi=gi, issue=issue: issue(gi + 2)) if (kt == 0 and m == 0) else None,
                                "qk": [(K[ps, j * 1024 + kt * 128:j * 1024 + (kt + 1) * 128], qT[ps, h, sl], [Kr, qres(h, qb)])],
                                "v": (V[:, j * 8 + kt, :], Vr), "m": m,
                                "first": ch == 0 and kt == 0, "last": ch == 7 and kt == 7})
                attn_steps(steps, [3, 4], [5, 6])
                rec0, rec0r = C.scratch()
                rec1, rec1r = C.scratch()
                o32, o32r = C.scratch()
                t32, t32r = C.scratch()
                P.op("dve", lambda e, rec0=rec0: e.reciprocal(out=rec0[:], in_=C.pb[5][:]), reads=["pb5"], writes=[rec0r])
                P.op("dve", lambda e, rec1=rec1: e.reciprocal(out=rec1[:], in_=C.pb[6][:]), reads=["pb6"], writes=[rec1r])
                P.op("dve", lambda e, rec0=rec0, o32=o32: e.tensor_tensor(out=o32[:], in0=C.pb[3][:], in1=rec0[:], op=ALU.mult),
                     reads=["pb3", rec0r], writes=[o32r])
                P.op("dve", lambda e, rec1=rec1, t32=t32: e.tensor_tensor(out=t32[:], in0=C.pb[4][:], in1=rec1[:], op=ALU.mult),
                     reads=["pb4", rec1r], writes=[t32r])
                P.op("dve", lambda e, o32=o32, t32=t32: e.scalar_tensor_tensor(
                    out=o32[:], in0=t32[:], scalar=neglam[:, 0:1], in1=o32[:], op0=ALU.mult, op1=ALU.add),
                    reads=[t32r, o32r, "neglam"], writes=[o32r])
                sq, sqr = C.scratch()
                P.op("act", lambda e, o32=o32, sq=sq: e.activation(out=sq[:], in_=o32[:], func=AF.Square), reads=[o32r], writes=[sqr])
                P.op("pe", lambda e, sq=sq: e.matmul(C.pb[7][:], lhsT=C.onesf[:], rhs=sq[:], start=True, stop=True),
                     reads=[sqr, "onesf"], writes=["pb7"])
                P.op("act", lambda e, sq=sq: e.activation(out=sq[:], in_=C.pb[7][:], func=AF.Sqrt, bias=C.epsT[:, 0:1],
                                                          scale=1.0 / 128.0), reads=["pb7", "epsT", sqr], writes=[sqr])
                P.op("dve", lambda e, sq=sq: e.reciprocal(out=sq[:], in_=sq[:]), reads=[sqr], writes=[sqr])
                P.op("dve", lambda e, o32=o32, sq=sq, h=h, sl=sl: e.scalar_tensor_tensor(
                    out=aT[:, h, sl], in0=o32[:], scalar=gsub[:, 0:1], in1=sq[:], op0=ALU.mult, op1=ALU.mult),
                    reads=[o32r, sqr, "gsub"], writes=[ares(h, qb)])
        gl["g"] = gbase + len(chunks)

        for dc in range(16):
            wb, wres = C.weight_block()
            for tb in range(2):
                sl = slice(tb * 512, (tb + 1) * 512)
                pbk, pres = C.bank(0, 7)
                for cc in range(16):
                    P.op("pe", lambda e, pbk=pbk, wb=wb, cc=cc, sl=sl: e.matmul(
                        pbk[:], lhsT=wb[:, cc, :], rhs=aT[:, cc, sl], start=(cc == 0), stop=(cc == 15)),
                        reads=[wres, ares(cc, tb)], writes=[pres])
                P.op("dve", lambda e, pbk=pbk, dc=dc, sl=sl: e.tensor_tensor(
                    out=xT[:, dc, sl], in0=pbk[:], in1=xT[:, dc, sl], op=ALU.add),
                    reads=[pres, xres(dc, tb)], writes=[xres(dc, tb)])

        rmsnorm_T(C, xT, xres, gM[:, l * 16:(l + 1) * 16], aT, ares, gres="gM")
        uT = qT
        for fg in range(4):
            for fc in range(16):
                f0 = (fg * 16 + fc) * 128
                wb, wres = C.weight_block()
                for tb in range(2):
                    sl = slice(tb * 512, (tb + 1) * 512)
                    pbk, pres = C.bank(0, 7)
                    for dc in range(16):
                        P.op("pe", lambda e, pbk=pbk, wb=wb, dc=dc, sl=sl: e.matmul(
                            pbk[:], lhsT=wb[:, dc, :], rhs=aT[:, dc, sl], start=(dc == 0), stop=(dc == 15)),
                            reads=[wres, ares(dc, tb)], writes=[pres])
                    r32, r32r = C.scratch()
                    P.op("act", lambda e, r32=r32, pbk=pbk: e.activation(out=r32[:], in_=pbk[:], func=AF.Relu),
                         reads=[pres], writes=[r32r])
                    P.op("dve", lambda e, r32=r32, fc=fc, sl=sl: e.tensor_tensor(
                        out=uT[:, fc, sl], in0=r32[:], in1=r32[:], op=ALU.mult), reads=[r32r], writes=[qres(fc, tb)])
            for dc in range(16):
                wb, wres = C.weight_block()
                for tb in range(2):
                    sl = slice(tb * 512, (tb + 1) * 512)
                    pbk, pres = C.bank(0, 7)
                    for fc in range(16):
                        P.op("pe", lambda e, pbk=pbk, wb=wb, fc=fc, sl=sl: e.matmul(
                            pbk[:], lhsT=wb[:, fc, :], rhs=uT[:, fc, sl], start=(fc == 0), stop=(fc == 15)),
                            reads=[wres, qres(fc, tb)], writes=[pres])
                    P.op("dve", lambda e, pbk=pbk, dc=dc, sl=sl: e.tensor_tensor(
                        out=xT[:, dc, sl], in0=pbk[:], in1=xT[:, dc, sl], op=ALU.add),
                        reads=[pres, xres(dc, tb)], writes=[xres(dc, tb)])

    inv = 1.0 / 2048.0
    for tb in range(2):
        sl = slice(tb * 512, (tb + 1) * 512)
        for dc in range(16):
            sq, sqr = C.scratch()
            P.op("act", lambda e, sq=sq, dc=dc, sl=sl: e.activation(out=sq[:], in_=xT[:, dc, sl], func=AF.Square),
                 reads=[xres(dc, tb)], writes=[sqr])
            P.op("pe", lambda e, sq=sq, dc=dc: e.matmul(C.pb[7][:], lhsT=C.onesf[:], rhs=sq[:], start=(dc == 0), stop=(dc == 15)),
                 reads=[sqr, "onesf"], writes=["pb7"])
        sd, sdr = C.scratch()
        P.op("act", lambda e, sd=sd: e.activation(out=sd[:], in_=C.pb[7][:], func=AF.Sqrt, bias=C.epsT[:, 0:1], scale=inv),
             reads=["pb7", "epsT"], writes=[sdr])
        P.op("dve", lambda e, sd=sd, tb=tb: e.reciprocal(out=C.rstd[tb][:], in_=sd[:]), reads=[sdr], writes=[f"rstd{tb}"])
        for dc in range(16):
            y, yr = C.scratch()
            P.op("dve", lambda e, y=y, dc=dc, sl=sl, tb=tb: e.scalar_tensor_tensor(
                out=y[:], in0=xT[:, dc, sl], scalar=gF[:, dc:dc + 1], in1=C.rstd[tb][:], op0=ALU.mult, op1=ALU.mult),
                reads=[xres(dc, tb), f"rstd{tb}", "gF"], writes=[yr])
            P.finals.append(P.dma(y_o[dc * 128:(dc + 1) * 128, sl], y[:], reads=[yr], key=f"yo{yr}"))
    P.emit()
    return nc

def _gains16(v):
    return np.ascontiguousarray(np.asarray(v, np.float32).reshape(-1, 128).T)


def _rope_tables(core):
    inv_freq = (1.0 / (np.float32(10000.0) ** (np.arange(0, 64, 2, dtype=np.float32) / np.float32(64)))).astype(np.float32)
    pos = (np.arange(TOK, dtype=np.float32) + np.float32(core * TOK)).astype(np.float32)
    ang = (pos[None, :] * inv_freq[:, None]).astype(np.float32)
    c = np.cos(ang).astype(np.float32)
    s = np.sin(ang).astype(np.float32)
    Cf = np.tile(c, (4, 1))
    sign = np.where((np.arange(128) % 64) < 32, -1.0, 1.0).astype(np.float32)[:, None]
    Sf = np.tile(s, (4, 1)) * sign
    return np.stack([Cf * np.float32(0.125), Sf * np.float32(0.125), Cf, Sf]).astype(np.float32)


def _perm():
    Pm = np.zeros((128, 128), np.float32)
    for m in range(128):
        k = m + 32 if (m % 64) < 32 else m - 32
        Pm[k, m] = 1.0
    return Pm


def _na_table(rpb_l):
    c = np.arange(64)
    cs = np.clip(c - 8, 0, 48)
    cp = np.arange(64)[:, None]
    valid = (cp >= cs[None, :]) & (cp < cs[None, :] + 16)
    rel = np.clip(cp - c[None, :] + 15, 0, 30)
    out = np.full((8, 128, NBLK, 64), NEG, np.float32)
    for b in range(NBLK):
        for half in range(2):
            dr = -(b - B0) - 4 + half
            if abs(dr) <= 7:
                blk = rpb_l[:, dr + 7, :][:, rel]
                blk = np.where(valid[None], blk, np.float32(NEG))
                out[:, half * 64:(half + 1) * 64, b, :] = blk
    return np.ascontiguousarray(out.reshape(8, 128, NBLK * 64))


def _rowsel():
    R = np.zeros((32, 12 * 128), np.float32)
    for j in range(12):
        R[2 * j, j * 128:j * 128 + 64] = 1.0
        R[2 * j + 1, j * 128 + 64:(j + 1) * 128] = 1.0
    return R


def _mq(core):
    M = np.zeros((32, TOK), np.float32)
    base = 16 * core - 4
    for i in range(16):
        r = 16 * core + i
        rs = min(max(r - 4, 0), 128 - 8)
        for wr in range(24):
            rp = base + wr
            ok = (rs <= rp < rs + 8)
            if not ok:
                M[wr, i * 64:(i + 1) * 64] = NEG
    return M


def _selI(core):
    S = np.zeros((128, 16, 128), np.float32)
    if core - 1 >= 0:
        S[:, core - 1, :] = np.eye(128, dtype=np.float32)
    if core + 1 <= 7:
        S[:, 8 + core + 1, :] = np.eye(128, dtype=np.float32)
    return np.ascontiguousarray(S.reshape(128, 16 * 128))


_NC_CACHE = {}


def _host_inputs(x, attn_norm, w_in, lambda_q1, lambda_k1, lambda_q2, lambda_k2, diff_subln, na_norm, na_rpb, w_out,
                 mlp_norm, w_mlp_in, w_mlp_out, final_norm, L=DEPTH):
    f = lambda a: np.asarray(a, np.float32)
    x = f(x)
    bf = ml_dtypes.bfloat16
    cores = list(range(NCORES))
    common = {
        "w_in": np.ascontiguousarray(f(w_in)[:L]), "w_out": np.ascontiguousarray(f(w_out)[:L]),
        "w_mlp_in": np.ascontiguousarray(f(w_mlp_in)[:L]), "w_mlp_out": np.ascontiguousarray(f(w_mlp_out)[:L]),
        "g_attn": np.ascontiguousarray(np.concatenate([_gains16(attn_norm[l]) for l in range(L)], axis=1)),
        "g_mlp": np.ascontiguousarray(np.concatenate([_gains16(mlp_norm[l]) for l in range(L)], axis=1)),
        "g_final": _gains16(final_norm),
        "subln": np.ascontiguousarray(f(diff_subln)[:L].T),
        "na_g": np.ascontiguousarray(np.concatenate([_gains16(na_norm[l]) for l in range(L)], axis=1)),
        "lam": np.ascontiguousarray(np.concatenate(
            [np.stack([f(lambda_q1[l]), f(lambda_k1[l]), f(lambda_q2[l]), f(lambda_k2[l])], axis=1) for l in range(L)], axis=1)),
        "perm": _perm(),
        "natab": np.ascontiguousarray(np.concatenate([_na_table(f(na_rpb[l])) for l in range(L)], axis=0)),
        "rowsel": _rowsel().astype(bf),
        "ident": np.eye(128, dtype=np.float32).astype(bf),
    }
    consts = np.zeros((128, 2 * L), np.float32)
    for l in range(L):
        lam_init = 0.8 - 0.6 * math.exp(-0.3 * l)
        consts[:, 2 * l] = lam_init
        consts[:, 2 * l + 1] = 1.0 - lam_init
    common["consts"] = consts
    ins = []
    for c in cores:
        d = dict(common)
        d["xT"] = np.ascontiguousarray(x[0, c * TOK:(c + 1) * TOK, :].T)
        d["rope"] = np.ascontiguousarray(_rope_tables(c)[2:4])
        d["mq"] = _mq(c).astype(bf)
        d["selI"] = _selI(c).astype(bf)
        ins.append(d)
    return ins


def kernel(x, attn_norm, w_in, lambda_q1, lambda_k1, lambda_q2, lambda_k2, diff_subln, na_norm, na_rpb, w_out,
           mlp_norm, w_mlp_in, w_mlp_out, final_norm):
    if "F" not in _NC_CACHE:
        _NC_CACHE["F"] = build_fused(DEPTH)
    ins = _host_inputs(x, attn_norm, w_in, lambda_q1, lambda_k1, lambda_q2, lambda_k2, diff_subln, na_norm, na_rpb, w_out,
                       mlp_norm, w_mlp_in, w_mlp_out, final_norm)
    res = run_bass_kernel_spmd(_NC_CACHE["F"], ins, core_ids=list(range(NCORES))).results
    out = np.concatenate([res[c]["yT"].T for c in range(NCORES)], axis=0)[None]
    return np.ascontiguousarray(out.astype(np.float32))
```
